# Optimizing a Trainium2 kernel written in Bass

```python
import jax, jax.numpy as jnp
from jax import lax
import numpy as np

D_MODEL = 1024
BATCH = 4
SEQ = 4096
DEPTH = 1

EPS = 1e-6
NEG = -1e30
CHUNK = 128
A_GROUPS = 4
A_WIDTH = D_MODEL
A_GROUP_W = A_WIDTH // A_GROUPS
B_PATTERNS = ((128, 1), (512, 4), (2048, 16))
B_NPAT = len(B_PATTERNS)
B_HEADS = 8
B_HEAD_DIM = 64
B_WIDTH = B_HEADS * B_HEAD_DIM
B_QKV = B_NPAT * 3 * B_WIDTH
N_IN = 3 * A_WIDTH + B_QKV + B_WIDTH + 2 * D_MODEL

kernel_name = "hybrid_gmlp_dilated_attn_gated_block"


def _rmsnorm(t, g):
    t32 = t.astype(jnp.float32)
    t32 = t32 * lax.rsqrt(jnp.mean(t32 * t32, axis=-1, keepdims=True) + EPS)
    return (t32 * g.astype(jnp.float32)).astype(t.dtype)


def _layernorm(t, g, b):
    t32 = t.astype(jnp.float32)
    mu = jnp.mean(t32, axis=-1, keepdims=True)
    var = jnp.mean(jnp.square(t32 - mu), axis=-1, keepdims=True)
    y = (t32 - mu) * lax.rsqrt(var + EPS) * g.astype(jnp.float32) + b.astype(jnp.float32)
    return y.astype(t.dtype)


def _gmlp_sgu(u, v, w_s, b_s, ln_g, ln_b):
    bsz, seq, _ = v.shape
    u = jax.nn.gelu(u)
    v = _layernorm(jax.nn.gelu(v), ln_g, ln_b)
    vc = v.reshape(bsz, seq // CHUNK, CHUNK, A_GROUPS, A_GROUP_W)
    mask = jnp.tril(jnp.ones((CHUNK, CHUNK), dtype=bool))
    w = jnp.where(mask[None], w_s, jnp.zeros_like(w_s))
    s = jnp.einsum('gij,bcjgh->bcigh', w, vc) + jnp.swapaxes(b_s, 0, 1)[None, None, :, :, None]
    return u * s.reshape(bsz, seq, A_WIDTH)


def _strided(t, d):
    bsz, seq = t.shape[:2]
    return jnp.swapaxes(t.reshape(bsz, seq // d, d, *t.shape[2:]), 1, 2)


def _unstrided(t, seq):
    bsz = t.shape[0]
    return jnp.swapaxes(t, 1, 2).reshape(bsz, seq, *t.shape[3:])


def _dilated_window(q, k, v, window, dilation):
    bsz, seq, nh, hd = q.shape
    n = window // dilation
    L = seq // dilation
    nb = -(-L // n)
    Lp = nb * n

    def blocks(t):
        t = _strided(t, dilation)
        t = jnp.pad(t, ((0, 0), (0, 0), (0, Lp - L), (0, 0), (0, 0)))
        return t.reshape(bsz, dilation, nb, n, nh, hd)

    def with_prev(t):
        prev = jnp.pad(t, ((0, 0), (0, 0), (1, 0), (0, 0), (0, 0), (0, 0)))[:, :, :-1]
        return jnp.concatenate([prev, t], axis=3)

    qb = blocks(q)
    kk = with_prev(blocks(k))
    vv = with_prev(blocks(v))
    s = jnp.einsum('bdcqhe,bdckhe->bdchqk', qb, kk).astype(jnp.float32) * (hd ** -0.5)
    qi = jnp.arange(n)[:, None]
    kj = jnp.arange(2 * n)[None, :]
    dist = qi + n - kj
    band = (dist >= 0) & (dist <= n)
    key_pos = jnp.arange(nb)[:, None] * n + jnp.arange(2 * n)[None, :] - n
    mask = band[None] & (key_pos >= 0)[:, None, :]
    s = jnp.where(mask[:, None], s, NEG)
    m = jnp.max(s, axis=-1, keepdims=True)
    e = jnp.exp(s - m)
    den = jnp.sum(e, axis=-1, keepdims=True)
    p = (e / den).astype(v.dtype)
    o = jnp.einsum('bdchqk,bdckhe->bdcqhe', p, vv)
    lse = jnp.swapaxes((m + jnp.log(den))[..., 0], 3, 4)
    o = o.reshape(bsz, dilation, Lp, nh, hd)[:, :, :L]
    lse = lse.reshape(bsz, dilation, Lp, nh)[:, :, :L]
    return _unstrided(o, seq), _unstrided(lse, seq)


def _dilated_mixture(qkv, qn_g, kn_g):
    outs, lses = [], []
    for p, (window, dilation) in enumerate(B_PATTERNS):
        q = _rmsnorm(qkv[:, :, p, 0], qn_g[p])
        k = _rmsnorm(qkv[:, :, p, 1], kn_g[p])
        o, lse = _dilated_window(q, k, qkv[:, :, p, 2], window, dilation)
        outs.append(o)
        lses.append(lse)
    wts = jax.nn.softmax(jnp.stack(lses, axis=0), axis=0)
    o = jnp.sum(wts[..., None].astype(qkv.dtype) * jnp.stack(outs, axis=0), axis=0)
    return o


def setup_inputs(seed: int = 0) -> dict:
    key = jax.random.key(seed)
    ks = jax.random.split(key, 12)
    f32 = jnp.float32
    x = jax.random.normal(ks[0], (BATCH, SEQ, D_MODEL), f32)
    norm_g = 1.0 + 0.02 * jax.random.normal(ks[1], (DEPTH, D_MODEL), f32)
    w_in = jax.random.normal(ks[2], (DEPTH, D_MODEL, N_IN), f32) * D_MODEL ** -0.5
    a_ws = jax.random.normal(ks[3], (DEPTH, A_GROUPS, CHUNK, CHUNK), f32) * CHUNK ** -0.5
    a_bs = 1.0 + 0.02 * jax.random.normal(ks[4], (DEPTH, A_GROUPS, CHUNK), f32)
    a_ln_g = 1.0 + 0.02 * jax.random.normal(ks[5], (DEPTH, A_WIDTH), f32)
    a_ln_b = 0.02 * jax.random.normal(ks[6], (DEPTH, A_WIDTH), f32)
    b_qn_g = 1.0 + 0.02 * jax.random.normal(ks[7], (DEPTH, B_NPAT, B_HEAD_DIM), f32)
    b_kn_g = 1.0 + 0.02 * jax.random.normal(ks[8], (DEPTH, B_NPAT, B_HEAD_DIM), f32)
    w_oa = jax.random.normal(ks[9], (DEPTH, A_WIDTH, D_MODEL), f32) * A_WIDTH ** -0.5
    w_ob = jax.random.normal(ks[10], (DEPTH, B_WIDTH, D_MODEL), f32) * B_WIDTH ** -0.5
    w_out = jax.random.normal(ks[11], (DEPTH, D_MODEL, D_MODEL), f32) * D_MODEL ** -0.5
    return {"x": x, "norm_g": norm_g, "w_in": w_in, "a_ws": a_ws, "a_bs": a_bs,
            "a_ln_g": a_ln_g, "a_ln_b": a_ln_b, "b_qn_g": b_qn_g, "b_kn_g": b_kn_g,
            "w_oa": w_oa, "w_ob": w_ob, "w_out": w_out}


def reference(x, norm_g, w_in, a_ws, a_bs, a_ln_g, a_ln_b, b_qn_g, b_kn_g, w_oa, w_ob, w_out):
    bsz, seq, _ = x.shape
    split_pts = np.cumsum([A_WIDTH, A_WIDTH, A_WIDTH, B_QKV, B_WIDTH, D_MODEL]).tolist()
    for l in range(DEPTH):
        h = _rmsnorm(x, norm_g[l])
        proj = jnp.einsum('bsd,dn->bsn', h, w_in[l])
        a_u, a_v, a_gate, b_qkv, b_gate, g_a, g_b = jnp.split(proj, split_pts, axis=-1)
        ya = _gmlp_sgu(a_u, a_v, a_ws[l], a_bs[l], a_ln_g[l], a_ln_b[l]) * jax.nn.silu(a_gate)
        ya = jnp.einsum('bsc,cd->bsd', ya, w_oa[l])
        qkv = b_qkv.reshape(bsz, seq, B_NPAT, 3, B_HEADS, B_HEAD_DIM)
        yb = _dilated_mixture(qkv, b_qn_g[l], b_kn_g[l]).reshape(bsz, seq, B_WIDTH)
        yb = jnp.einsum('bsc,cd->bsd', yb * jax.nn.silu(b_gate), w_ob[l])
        merged = jax.nn.sigmoid(g_a) * ya + jax.nn.sigmoid(g_b) * yb
        x = x + jnp.einsum('bsd,de->bse', merged, w_out[l])
    return x
```

```python
import numpy as np
import concourse.bass as bass
import concourse.mybir as mybir
from concourse.bass_utils import run_bass_kernel_spmd

F32 = mybir.dt.float32
BF16 = mybir.dt.bfloat16
ALU = mybir.AluOpType
AF = mybir.ActivationFunctionType


class Buf:
    __slots__ = ("name", "w", "r", "dsem", "dcnt")

    def __init__(self, name):
        self.name = name
        self.w = None
        self.r = []
        self.dsem = None
        self.dcnt = 0


class Eng:
    def __init__(self, name, h, sem):
        self.name = name
        self.h = h
        self.sem = sem
        self.cnt = 0
        self.seen = {}


class Ctx:
    def __init__(self, nc, needed=None):
        self.nc = nc
        self.needed = needed
        self.rank = None if needed is None else {k: {v: i + 1 for i, v in enumerate(sorted(vs))} for k, vs in needed.items()}
        self.waited = {}
        self.stack = []
        self.nsem = 0
        self.pe = self._eng("pe", nc.tensor)
        self.act = self._eng("act", nc.scalar)
        self.dve = self._eng("dve", nc.vector)
        self.pool = self._eng("pool", nc.gpsimd)
        self.sp = self._eng("sp", nc.sync)
        self.engs = [self.pe, self.act, self.dve, self.pool, self.sp]
        self.dbufs = []

    def new_sem(self, name):
        cm = self.nc.semaphore(name)
        s = cm.__enter__()
        self.stack.append(cm)
        self.nsem += 1
        return s

    def _eng(self, name, h):
        return Eng(name, h, self.new_sem("s_" + name))

    def close(self):
        for cm in reversed(self.stack):
            cm.__exit__(None, None, None)
        self.stack = []

    def _wait(self, eng, toks):
        best = {}
        for t in toks:
            if t is None:
                continue
            sem, val, src = t
            if src is self.pe and eng is self.pe:
                continue
            k = id(sem)
            if eng.seen.get(k, 0) >= val:
                continue
            if k not in best or best[k][1] < val:
                best[k] = (sem, val, src)
        for k, (sem, val, src) in best.items():
            if src is None:
                actual = val
            else:
                self.waited.setdefault(src.name, set()).add(val)
                actual = val if self.rank is None else self.rank[src.name][val]
            eng.h.wait_ge(sem, actual)
            eng.seen[k] = val

    def _deps(self, reads, writes):
        toks = []
        for b in reads:
            toks.append(b.w)
        for b in writes:
            toks.append(b.w)
            toks.extend(b.r)
        return toks

    def op(self, eng, fn, reads=(), writes=(), inc=True):
        self._wait(eng, self._deps(reads, writes))
        ins = fn()
        if inc:
            eng.cnt += 1
            if self.needed is None or eng.cnt in self.needed.get(eng.name, ()):
                ins.then_inc(eng.sem, 1)
            tok = (eng.sem, eng.cnt, eng)
        else:
            tok = (eng.sem, eng.cnt + 1, eng)
        for b in reads:
            b.r.append(tok)
        for b in writes:
            b.w = tok
            b.r = []
        return tok

    def dma(self, eng, out, in_, reads=(), writes=(), **kw):
        self._wait(eng, self._deps(reads, writes))
        owner = writes[0] if writes else reads[0]
        if owner.dsem is None:
            owner.dsem = self.new_sem("d_" + owner.name)
            self.dbufs.append(owner)
        owner.dcnt += 1
        eng.h.dma_start(out=out, in_=in_, **kw).then_inc(owner.dsem, 16)
        tok = (owner.dsem, 16 * owner.dcnt, None)
        for b in reads:
            b.r.append(tok)
        for b in writes:
            b.w = tok
            b.r = []
        return tok

    def barrier(self):
        toks = [(e.sem, e.cnt, e) for e in self.engs if e.cnt > 0]
        toks += [(b.dsem, 16 * b.dcnt, None) for b in self.dbufs]
        for e in self.engs:
            self._wait(e, [t for t in toks if t[2] is not e])

    def drain(self, eng, bufs):
        toks = []
        for b in bufs:
            toks.append(b.w)
            toks.extend(b.r)
        self._wait(eng, toks)


class Arena:
    BASE = 16512
    TOP = 229344

    def __init__(self, nc):
        self.nc = nc
        self.limit = self.TOP - self.BASE
        self.n = 0
        self.live = []

    def alloc(self, name, shape, dtype, off):
        esz = 2 if dtype == BF16 else 4
        nbytes = int(np.prod(shape[1:])) * esz
        assert off % 32 == 0, (name, off)
        end = off + nbytes
        assert end <= self.limit, (name, off, nbytes, self.limit)
        self.n += 1
        t = self.nc.alloc_sbuf_tensor_at(f"{name}_{self.n}", list(shape), dtype, offset=self.BASE + off)
        b = Buf(name)
        keep = []
        for (s0, e0, ob) in self.live:
            if s0 < end and off < e0:
                if ob.w is not None:
                    b.r.append(ob.w)
                b.r.extend(ob.r)
            else:
                keep.append((s0, e0, ob))
        keep.append((off, end, b))
        self.live = keep
        return t, b, off + ((nbytes + 31) // 32) * 32


class FreeList:
    def __init__(self, items):
        self.free = list(items)

    def get(self):
        assert self.free, "pool exhausted (pipeline too deep for the buffer count)"
        return self.free.pop(0)

    def put(self, it):
        self.free.append(it)


class Pipe:
    def __init__(self):
        self.i = 0
        self.pend = []

    def submit(self, s1, s2=None, lag=1, tag=None):
        s1()
        due = [e for e in self.pend if e[0] <= self.i]
        self.pend = [e for e in self.pend if e[0] > self.i]
        for e in due:
            e[1]()
        if s2 is not None:
            self.pend.append((self.i + lag, s2, tag))
        self.i += 1

    def flush(self, tag=None):
        keep = []
        for e in self.pend:
            if tag is None or e[2] == tag:
                e[1]()
            else:
                keep.append(e)
        self.pend = keep


D = 1024
T = 2048
NCH = T // 128
N_IN = 10240
EPS = 1e-6
PATTERNS = ((128, 1), (512, 4), (2048, 16))
COL_U, COL_V, COL_G = 0, 1024, 2048
COL_QKV = 3072
COL_BG = 3072 + 4608
COL_GA = COL_BG + 512
COL_GB = COL_GA + 1024
C_MN, C_MH, C_TRIL, C_ID, C_BD, C_ONE = 0, 512, 1024, 1152, 1280, 1408
C_W = 1536
F_LNG, F_LNB, F_GQ, F_GK, F_EPS, F_EPS64, F_W = 0, 8, 16, 19, 22, 23, 24


def ssl(start, n, step):
    return slice(start, start + (n - 1) * step + 1, step)


def build_program(needed=None):
    nc = bass.Bass("TRN2", target_bir_lowering=False)
    dt_in = lambda name, shape: nc.dram_tensor(name, list(shape), F32, kind="ExternalInput").ap()
    xo = dt_in("xo", (T, D))
    xh = dt_in("xh", (T, D))
    w_in = dt_in("w_in", (D, N_IN))
    w_oa = dt_in("w_oa", (D, D))
    w_ob = dt_in("w_ob", (512, D))
    w_out = dt_in("w_out", (D, D))
    g_bc = dt_in("g_bc", (128, D))
    aws = dt_in("aws", (128, 4, 128))
    bsb = dt_in("bsb", (128, 4, 128))
    cbf = dt_in("cbf", (128, C_W))
    cf = dt_in("cf", (128, F_W))
    out = nc.dram_tensor("out", [T, D], F32, kind="ExternalOutput").ap()

    c = Ctx(nc, needed)
    ar = Arena(nc)
    pipe = Pipe()
    PE, ACT, DVE, POOL, SP = c.pe, c.act, c.dve, c.pool, c.sp
    tn, sc, ve, gp = nc.tensor, nc.scalar, nc.vector, nc.gpsimd

    def w3(ap2d, c0, ncol):
        return ap2d.rearrange("(kc p) c -> p kc c", p=128)[:, :, c0:c0 + ncol]

    def mkpool(prefix, n, shape, dtype, off):
        items = []
        for i in range(n):
            t_, b_, off = ar.alloc(f"{prefix}{i}", shape, dtype, off)
            items.append((t_, b_))
        return FreeList(items), off

    banks = FreeList([(nc.alloc_psum_tensor(f"pb{i}", [128, 512], F32), Buf(f"pb{i}")) for i in range(8)])

    off = 0
    cb_t, b_cb, off = ar.alloc("cbf", [128, C_W], BF16, off)
    cf_t, b_cf, off = ar.alloc("cf", [128, F_W], F32, off)
    ybT, b_ybT, off = ar.alloc("ybT", [128, 4, T], BF16, off)
    hTo, _b_unused, off = ar.alloc("hTo", [128, 8, T], BF16, off)
    b_hTo = [Buf(f"hTo{g}") for g in range(4)]
    NW = 8
    wring = []
    for i in range(NW):
        t_, b_, off = ar.alloc(f"wr{i}", [128, 8, 128], BF16, off)
        wring.append((t_, b_))
    P_BASE = off
    wstate = {"i": 0}

    c.dma(SP, cf_t[:], cf, writes=[b_cf])
    maskN = cb_t[:, C_MN:C_MN + 512]
    maskH = cb_t[:, C_MH:C_MH + 512]
    trilW = cb_t[:, C_TRIL:C_TRIL + 128]
    ident = cb_t[:, C_ID:C_ID + 128]
    bdones = cb_t[:, C_BD:C_BD + 128]
    ones_m = cb_t[:, C_ONE:C_ONE + 128]
    eps_c = cf_t[:, F_EPS:F_EPS + 1]
    eps64_c = cf_t[:, F_EPS64:F_EPS64 + 1]

    def load_w(c0, src=None, rows=8):
        t_, b_ = wring[wstate["i"] % NW]
        wstate["i"] += 1
        src = w_in if src is None else src
        c.dma(POOL, t_[:, 0:rows, :], w3(src, c0, 128), writes=[b_])
        return t_, b_

    def mm_group(ps, pbuf, lhs_fn, rhs_fn, nk, reads, inc_last=True):
        for kc in range(nk):
            c.op(PE, lambda kc=kc: tn.matmul(ps, lhs_fn(kc), rhs_fn(kc), start=(kc == 0), stop=(kc == nk - 1)),
                 reads=reads, writes=[pbuf], inc=(kc == nk - 1 and inc_last))

    hTh, _b_unused2, off = ar.alloc("hTh", [128, 8, T], BF16, P_BASE)
    b_hTh = [Buf(f"hTh{i}") for i in range(NCH)]
    ar.live.extend((P_BASE, P_BASE + 32768, b_) for b_ in b_hTh)
    PB_BASE = off
    qT, b_qT, off = ar.alloc("qT", [128, T], BF16, off)
    kT, b_kT, off = ar.alloc("kT", [128, 2 * T], BF16, off)
    sq_pool, off = mkpool("sq", 4, [128, 512], BF16, off)
    lr_pool, off = mkpool("lr", 3, [128, 512], F32, off)
    Vaug, b_V, off = ar.alloc("Vaug", [128, 32, 2, 128], BF16, off)
    b_Vg = [Buf(f"Vg{g}") for g in range(8)]
    _vs = [e for e in ar.live if e[2] is b_V][0]
    ar.live.extend((_vs[0], _vs[1], b_) for b_ in b_Vg)
    c.op(DVE, lambda: ve.memset(Vaug[:], 1.0), writes=b_Vg)
    P0S_BASE = off
    gbc_t, b_gbc, off = ar.alloc("gbc", [128, D], F32, off)
    c.dma(ACT, gbc_t[:], g_bc, writes=[b_gbc])
    xt_pool, off = mkpool("xt", 4, [128, D], F32, off)
    junk, b_junk, off = ar.alloc("junk", [128, D], BF16, off)
    hb_pool, off = mkpool("hb", 4, [128, D], BF16, off)
    st_pool, off = mkpool("st", 3, [128, 4], F32, off)
    ACC0_BASE = off
    acc, b_acc, off = ar.alloc("acc", [128, 2, T], F32, off)
    tg_pool, off = mkpool("tg", 2, [128, 512], F32, off)
    SG_BASE = off
    xtx_pool, off = mkpool("xtx", 2, [128, D], F32, off)
    P0 = {"sg": None}
    own_pool = FreeList(xt_pool.free + xtx_pool.free)
    ty_pool, off = mkpool("ty", 2, [128, 512], F32, off)
    E_BASE = off
    E_pool, off = mkpool("E", 4, [128, 2, 2, 2, 128], BF16, off)
    E_extra = {}
    for (_t, _b) in E_pool.free:
        _e = [e for e in ar.live if e[2] is _b][0]
        E_extra[id(_b)] = Buf(_b.name + "b")
        ar.live.append((_e[0], _e[1], E_extra[id(_b)]))
    pa = list(Vaug[:].ap[0])
    pacc = list(acc[:].ap[0])
    ACC = {0: (acc, b_acc), 1: (acc, b_acc)}
    vstate = {"i": 0}

    def p0_task(ci, halo):
        src = xh if halo else xo
        dstT = hTh if halo else hTo
        dstB = b_hTh[ci] if halo else b_hTo[ci // 4]
        S = {}

        def s1():
            xt, b_xt = S["xt"] = (xt_pool if halo else own_pool).get()
            hb, b_hb = S["hb"] = hb_pool.get()
            st, b_st = st_pool.get()
            c.dma(SP, xt[:], src[ci * 128:(ci + 1) * 128, :], writes=[b_xt])
            c.op(ACT, lambda: sc.activation(junk[:], xt[:], AF.Square, accum_out=st[:, 0:1]),
                 reads=[b_xt], writes=[b_junk, b_st])
            c.op(ACT, lambda: sc.activation(st[:, 1:2], st[:, 0:1], AF.Ln, bias=eps_c, scale=1.0 / D),
                 reads=[b_st, b_cf], writes=[b_st])
            c.op(ACT, lambda: sc.activation(st[:, 2:3], st[:, 1:2], AF.Exp, scale=-0.5),
                 reads=[b_st], writes=[b_st])
            c.op(DVE, lambda: ve.scalar_tensor_tensor(hb[:], xt[:], st[:, 2:3], gbc_t[:], ALU.mult, ALU.mult),
                 reads=[b_xt, b_st, b_gbc], writes=[b_hb])
            (xt_pool if halo else own_pool).put(S["xt"])
            st_pool.put((st, b_st))

        def s2():
            hb, b_hb = S["hb"]
            bk, b_bk = banks.get()
            tv = bk[:].bitcast(BF16).rearrange("p (k t) -> p k t", k=8)
            for kc in range(8):
                c.op(PE, lambda kc=kc: tn.transpose(tv[:, kc, :], hb[:, kc * 128:(kc + 1) * 128], ident),
                     reads=[b_hb, b_cb], writes=[b_bk], inc=(kc == 7))
            c.op(DVE, lambda: ve.tensor_copy(dstT[:, :, ci * 128:(ci + 1) * 128], tv), reads=[b_bk], writes=[dstB])
            hb_pool.put(S["hb"])
            banks.put((bk, b_bk))

        return lambda: pipe.submit(s1, s2, lag=(4 if (halo and ci < 15) else 2))

    def hTh_bufs(t0, n):
        return [b_hTh[i] for i in range(t0 // 128, (t0 + n - 1) // 128 + 1)]

    def hTo_bufs(t0, n):
        return [b_hTo[i] for i in range(t0 // 512, (t0 + n - 1) // 512 + 1)]

    QKLAG = [2]

    def qk_task(src_fn, w_t, b_w, srcBs, n, d, gcol, dst, dstB):
        S = {}

        def s1():
            ps, b_ps = S["ps"] = banks.get()
            mm_group(ps[:, 0:n], b_ps, lambda kc: w_t[:, kc, :], src_fn, 8, [b_w] + srcBs)
            sq, b_sq = S["sq"] = sq_pool.get()
            c.op(ACT, lambda: sc.activation(sq[:, 0:n], ps[:, 0:n], AF.Square), reads=[b_ps], writes=[b_sq])

        def s2():
            ps, b_ps = S["ps"]
            sq, b_sq = S["sq"]
            ps2, b_ps2 = banks.get()
            lr, b_lr = lr_pool.get()
            c.op(PE, lambda: tn.matmul(ps2[:, 0:n], bdones, sq[:, 0:n], start=True, stop=True),
                 reads=[b_sq, b_cb], writes=[b_ps2])
            c.op(ACT, lambda: sc.activation(lr[:, 0:n], ps2[:, 0:n], AF.Ln, bias=eps64_c, scale=1.0),
                 reads=[b_ps2, b_cf], writes=[b_lr])
            c.op(ACT, lambda: sc.activation(lr[:, 0:n], lr[:, 0:n], AF.Exp, scale=-0.5), reads=[b_lr], writes=[b_lr])
            pv_ = ps[:, 0:n].rearrange("p (j r) -> p r j", r=d)
            rv_ = lr[:, 0:n].rearrange("p (j r) -> p r j", r=d)
            c.op(DVE, lambda: ve.scalar_tensor_tensor(dst, pv_, gcol, rv_, ALU.mult, ALU.mult),
                 reads=[b_ps, b_lr, b_cf], writes=[dstB])
            banks.put(S["ps"]); banks.put((ps2, b_ps2)); sq_pool.put(S["sq"]); lr_pool.put((lr, b_lr))

        return lambda: pipe.submit(s1, s2, lag=QKLAG[0], tag="qk")

    stepw = {}

    def step_w(hp, p):
        if (hp, p) not in stepw:
            cq = COL_QKV + p * 1536 + hp * 128
            stepw[(hp, p)] = (load_w(cq), load_w(cq + 512), load_w(cq + 1024))
        return stepw[(hp, p)]

    finw = {}

    def fin_w(hp):
        if hp not in finw:
            finw[hp] = load_w(COL_BG + hp * 128)
        return finw[hp]

    vkinds = set()

    QK = {0: (qT, b_qT, kT, b_kT)}

    def step_tasks(hp, p, prefetch=True, buf=0, att_lag=3):
        qT, b_qT, kT, b_kT = QK[buf]
        acc, b_acc = ACC[hp % 2]
        tasks = []
        va = []
        if p == 2:
            fin_w(hp)
        if True:
            win, d = PATTERNS[p]
            nbo = NCH // d
            LK = 128 + T // d
            (wq, b_wq), (wk, b_wk), (wv, b_wv) = step_w(hp, p)
            nxt = hp * 3 + p + 1
            if nxt < 12 and prefetch:
                step_w(nxt // 3, nxt % 3)
            qv = qT[:].rearrange("p (r j) -> p r j", r=d)
            kv = kT[:, 0:d * LK].rearrange("p (r j) -> p r j", r=d)
            gq = cf_t[:, F_GQ + p:F_GQ + p + 1]
            gk = cf_t[:, F_GK + p:F_GK + p + 1]
            for g in range(4):
                j0 = g * 512 // d
                tasks.append(qk_task(lambda kc, g=g: hTo[:, kc, g * 512:(g + 1) * 512], wq, b_wq, [b_hTo[g]], 512, d, gq,
                                     qv[:, :, j0:j0 + 512 // d], b_qT))
            nh = 128 * d
            for s0 in range(0, nh, 512):
                n = min(512, nh - s0)
                t0 = T - nh + s0
                j0 = s0 // d
                tasks.append(qk_task(lambda kc, t0=t0, n=n: hTh[:, kc, t0:t0 + n], wk, b_wk, hTh_bufs(t0, n), n, d, gk,
                                     kv[:, :, j0:j0 + n // d], b_kT))
            for g in range(4):
                j0 = 128 + g * 512 // d
                tasks.append(qk_task(lambda kc, g=g: hTo[:, kc, g * 512:(g + 1) * 512], wk, b_wk, [b_hTo[g]], 512, d, gk,
                                     kv[:, :, j0:j0 + 512 // d], b_kT))
            blocks = [(r, cb) for r in range(d) for cb in range(nbo + 1)]
            vtasks, atasks = [], []
            for b0 in range(0, len(blocks), 4):
                def v_s1(b0=b0):
                    grpb = blocks[b0:b0 + 4]
                    ps, b_ps = banks.get()
                    for bi, (r, cb) in enumerate(grpb):
                        if cb == 0:
                            srcT, t0 = hTh, T - 128 * d + r
                            srcBs = hTh_bufs(T - 128 * d, 128 * d)
                        else:
                            srcT, t0 = hTo, r + 128 * d * (cb - 1)
                            srcBs = hTo_bufs(128 * d * (cb - 1), 128 * d)
                        mm_group(ps[:, bi * 128:(bi + 1) * 128], b_ps,
                                 lambda kc: srcT[:, kc, ssl(t0, 128, d)], lambda kc: wv[:, kc, :], 8, [b_wv] + srcBs,
                                 inc_last=(bi == len(grpb) - 1))
                    nb = len(grpb)
                    dst = bass.AP(Vaug, b0 * 256, [pa, [256, nb], [192, 2], [1, 64]])
                    srcv = ps[:, 0:nb * 128].rearrange("p (b h e) -> p b h e", b=nb, h=2)
                    c.op(DVE, lambda: ve.tensor_copy(dst, srcv), reads=[b_ps], writes=[b_Vg[b0 // 4]])
                    banks.put((ps, b_ps))
                _f = (lambda v_s1=v_s1: pipe.submit(v_s1))
                vkinds.add(id(_f))
                vtasks.append(_f)
            qblocks = [(r, cc) for r in range(d) for cc in range(nbo)]
            for q0 in range(0, len(qblocks), 2):
                def att(q0=q0):
                    pair = qblocks[q0:q0 + 2]
                    S = {}

                    def s1():
                        E, b_E0 = S["E"] = E_pool.get()
                        b_Eb = [b_E0, E_extra[id(b_E0)]]
                        sb = [banks.get(), banks.get()]
                        for bi, (r, cc) in enumerate(pair):
                            for pc in range(2):
                                for h in range(2):
                                    rows = slice(64 * h, 64 * h + 64)
                                    k0 = 128 * (cc + pc)
                                    o_ = (bi * 2 + pc) * 128
                                    c.op(PE, lambda: tn.matmul(sb[h][0][:, o_:o_ + 128], kv[rows, r, k0:k0 + 128],
                                                               qv[rows, r, 128 * cc:128 * cc + 128], start=True, stop=True),
                                         reads=[b_kT, b_qT], writes=[sb[h][1]], inc=(bi == len(pair) - 1 and pc == 1))
                        for h in range(2):
                            c.op(ACT, lambda: sc.activation(E[:, h].rearrange("p b c q -> p (b c q)"), sb[h][0][:],
                                                            AF.Exp, scale=8.0),
                                 reads=[sb[h][1]], writes=b_Eb)
                        for bi, (r, cc) in enumerate(pair):
                            mk = maskH if cc == 0 else maskN
                            ev = E[:, :, bi].rearrange("p h c q -> p h (c q)")
                            if False:
                                c.op(POOL, lambda: gp.tensor_tensor(ev, ev, mk.rearrange("p (h x) -> p h x", h=2), ALU.mult),
                                     reads=[b_Eb[bi], b_cb], writes=[b_Eb[bi]])
                            else:
                                c.op(DVE, lambda: ve.tensor_tensor(ev, ev, mk.rearrange("p (h x) -> p h x", h=2), ALU.mult),
                                     reads=[b_Eb[bi], b_cb], writes=[b_Eb[bi]])
                        banks.put(sb[0]); banks.put(sb[1])

                    def s2():
                        E, b_E0 = S["E"]
                        b_Eb = [b_E0, E_extra[id(b_E0)]]
                        po, b_po = banks.get()
                        for bi, (r, cc) in enumerate(pair):
                            for h in range(2):
                                for pc in range(2):
                                    blk = r * (nbo + 1) + cc + pc
                                    o_ = (bi * 2 + h) * 128
                                    c.op(PE, lambda: tn.matmul(po[:, o_:o_ + 128], Vaug[:, blk, h, :], E[:, h, bi, pc, :],
                                                               start=(pc == 0), stop=(pc == 1)),
                                         reads=[b_Vg[blk // 4], b_Eb[bi]], writes=[b_po],
                                         inc=(pc == 1 and h == 1 and bi == len(pair) - 1))
                        (r0, c0), (r1, c1) = pair
                        t00 = r0 + 128 * d * c0
                        t01 = r1 + 128 * d * c1
                        av = bass.AP(acc, t00, [pacc, [t01 - t00, 2], [T, 2], [d, 128]])
                        ov = po[:].rearrange("p (b h q) -> p b h q", b=2, h=2)
                        if p == 0:
                            c.op(DVE, lambda: ve.tensor_copy(av, ov), reads=[b_po], writes=[b_acc])
                        else:
                            c.op(DVE, lambda: ve.tensor_tensor(av, ov, av, ALU.add), reads=[b_po, b_acc], writes=[b_acc])
                        E_pool.put(S["E"]); banks.put((po, b_po))

                    need = max(r * (nbo + 1) + cc + 1 for (r, cc) in pair) // 4
                    return need, (lambda: pipe.submit(s1, s2, lag=att_lag, tag="att"))
                atasks.append(att())
            nv = 0
            for need, a in atasks:
                while nv < len(vtasks) and nv <= need:
                    va.append(vtasks[nv]); nv += 1
                va.append(a)
            while nv < len(vtasks):
                va.append(vtasks[nv]); nv += 1
        return tasks, va

    def fin_tasks(hp, hook=None):
        wb, b_wb = fin_w(hp)
        acc, b_acc = ACC[hp % 2]
        SS = [dict() for _ in range(4)]
        first, second = [], []
        for g in range(4):
            def s1(g=g):
                S = SS[g]
                tsl = slice(g * 512, (g + 1) * 512)
                ps, b_ps = banks.get()
                mm_group(ps[:], b_ps, lambda kc: wb[:, kc, :], lambda kc: hTo[:, kc, tsl], 8, [b_wb, b_hTo[g]])
                tg, b_tg = tg_pool.get()
                sg, b_sg = S["sg"] = P0["sg"].get()
                c.op(ACT, lambda: sc.activation(tg[:], ps[:], AF.Tanh, scale=0.5), reads=[b_ps], writes=[b_tg])
                c.op(DVE, lambda: ve.scalar_tensor_tensor(sg[:], tg[:], 1.0, ps[:], ALU.add, ALU.mult),
                     reads=[b_tg, b_ps], writes=[b_sg])
                banks.put((ps, b_ps)); tg_pool.put((tg, b_tg))
            first.append(lambda s1=s1: pipe.submit(s1))
        first.append(lambda: pipe.flush(tag="att"))
        if hook is not None:
            first.append(hook)
        for g in range(4):
            def s2(g=g):
                S = SS[g]
                tsl = slice(g * 512, (g + 1) * 512)
                sg, b_sg = S["sg"]
                ty, b_ty = ty_pool.get()
                c.op(ACT, lambda: sc.activation(ty[0:64, :], acc[64:128, 0, tsl], AF.Ln), reads=[b_acc], writes=[b_ty])
                c.op(ACT, lambda: sc.activation(ty[64:128, :], acc[0:64, 1, tsl], AF.Ln), reads=[b_acc], writes=[b_ty])
                c.op(ACT, lambda: sc.activation(ty[:], ty[:], AF.Exp, scale=-1.0), reads=[b_ty], writes=[b_ty])
                c.op(DVE, lambda: ve.tensor_tensor(ty[0:64, :], ty[0:64, :], acc[0:64, 0, tsl], ALU.mult),
                     reads=[b_acc, b_ty], writes=[b_ty])
                c.op(DVE, lambda: ve.tensor_tensor(ty[64:128, :], ty[64:128, :], acc[64:128, 1, tsl], ALU.mult),
                     reads=[b_acc, b_ty], writes=[b_ty])
                c.op(DVE, lambda: ve.scalar_tensor_tensor(ybT[:, hp, tsl], ty[:], 0.5, sg[:], ALU.mult, ALU.mult),
                     reads=[b_ty, b_sg], writes=[b_ybT])
                P0["sg"].put(S["sg"]); ty_pool.put((ty, b_ty))
            second.append(lambda s2=s2: pipe.submit(s2))
        return first, second

    c.dma(POOL, cb_t[:], cbf, writes=[b_cb])
    own = [p0_task(ci, False) for ci in range(NCH)]
    halo = {hc: p0_task(hc, True) for hc in range(NCH)}
    q00, va00 = step_tasks(0, 0, prefetch=False)
    seq = [own[0], own[1], own[2], own[3], own[4], own[5], own[6], own[7], q00[0],
           own[8], own[9], own[10], own[11], q00[1],
           own[12], own[13], own[14], own[15], q00[2],
           halo[15], halo[14], q00[3]]
    for i, t in enumerate(seq):
        t()
    step_w(0, 1)
    P0["sg"], _ = mkpool("sg", 4, [128, 512], F32, SG_BASE)
    pend_halo = [13, 12]
    for i, t in enumerate(q00[4:] + va00):
        t()
        if i % 4 == 3 and pend_halo:
            halo[pend_halo.pop(0)]()
    while pend_halo:
        halo[pend_halo.pop(0)]()
    pend_halo = list(range(11, -1, -1))
    q01, va01 = step_tasks(0, 1)
    for i, t in enumerate(q01 + va01):
        t()
        if i % 2 == 1 and pend_halo:
            halo[pend_halo.pop(0)]()
    while pend_halo:
        halo[pend_halo.pop(0)]()
    pipe.flush()
    qT1, b_qT1, off2 = ar.alloc("qT1", [128, T], BF16, P0S_BASE)
    kT1, b_kT1, off2 = ar.alloc("kT1", [128, 2 * T], BF16, off2)
    QK[1] = (qT1, b_qT1, kT1, b_kT1)
    acc1, b_acc1, off2 = ar.alloc("acc1", [128, 2, T], F32, off2)
    ACC[1] = (acc1, b_acc1)

    A = {}

    def alloc_Wv():
        A["Wv"], A["b_Wv"], _ = ar.alloc("Wv", [128, 8, 1024], BF16, ACC0_BASE)
        for n in range(2):
            c.dma(POOL, A["Wv"][:, :, n * 512:(n + 1) * 512], w3(w_in, COL_V + n * 512, 512), writes=[A["b_Wv"]])

    def a0_setup():
        pipe.flush()
        WT, b_WT, off = ar.alloc("WT", [128, 4, 128], BF16, E_BASE)
        cst, b_cst, off = ar.alloc("cst", [128, 8, 128], F32, off)
        aws_t, b_aws, off = ar.alloc("aws", [128, 4, 128], F32, PB_BASE)
        bsb_t, b_bsb, off = ar.alloc("bsb", [128, 4, 128], F32, off)
        wm_t, b_wm, off = ar.alloc("wm", [128, 4, 128], BF16, off)
        A.update(WT=WT, b_WT=b_WT, cst=cst, b_cst=b_cst)
        c.dma(SP, aws_t[:], aws, writes=[b_aws])
        c.dma(SP, bsb_t[:], bsb, writes=[b_bsb])
        A["a0"] = (WT, b_WT, cst, b_cst, aws_t, b_aws, bsb_t, b_bsb, wm_t, b_wm)

    def a0_compute(part):
        WT, b_WT, cst, b_cst, aws_t, b_aws, bsb_t, b_bsb, wm_t, b_wm = A["a0"]
        if part == 0:
            for g in range(4):
                c.op(DVE, lambda: ve.tensor_tensor(wm_t[:, g, :], aws_t[:, g, :], trilW, ALU.mult),
                     reads=[b_aws, b_cb], writes=[b_wm])
        elif part == 1:
            tbk, b_tbk = banks.get()
            tbv = tbk[:].bitcast(BF16).rearrange("p (k t) -> p k t", k=8)
            for g in range(4):
                c.op(PE, lambda: tn.transpose(tbv[:, g, :], wm_t[:, g, :], ident), reads=[b_wm, b_cb], writes=[b_tbk],
                     inc=(g == 3))
            c.op(DVE, lambda: ve.tensor_copy(WT[:], tbv[:, 0:4, :]), reads=[b_tbk], writes=[b_WT])
            banks.put((tbk, b_tbk))
        elif part == 2:
            ps, b_ps = A["a0ps"] = banks.get()
            for g in range(4):
                c.op(PE, lambda: tn.matmul(ps[:, g * 128:(g + 1) * 128], ones_m, WT[:, g, :], start=True, stop=True),
                     reads=[b_WT, b_cb], writes=[b_ps], inc=(g == 3))
        else:
            ps, b_ps = A["a0ps"]
            for j in range(8):
                g = j // 2
                c.op(DVE, lambda: ve.scalar_tensor_tensor(cst[:, j, :], ps[:, g * 128:(g + 1) * 128],
                                                          cf_t[:, F_LNB + j:F_LNB + j + 1], bsb_t[:, g, :], ALU.mult, ALU.add),
                     reads=[b_ps, b_cf, b_bsb], writes=[b_cst])
            banks.put((ps, b_ps))

    steps = [(n // 3, n % 3) for n in range(12)]
    QKLAG[0] = 2
    cur_q, cur_va = step_tasks(0, 2, buf=0, att_lag=5)
    for t in cur_q:
        t()
    pend_fin = []
    for n in range(2, 12):
        hp, p = steps[n]
        if n + 1 < 12:
            nq, nva = step_tasks(steps[n + 1][0], steps[n + 1][1], buf=(n + 1) % 2, att_lag=(5 if n + 2 < 12 else 3))
        else:
            nq, nva = [], []
        lastv = max(i for i, t in enumerate(cur_va) if id(t) in vkinds)
        pipe.flush(tag="qk")
        qi = 0
        for i, t in enumerate(cur_va):
            t()
            if qi < len(nq):
                nq[qi](); qi += 1
            if i % 2 == 1 and pend_fin:
                pend_fin.pop(0)()
        while qi < len(nq):
            nq[qi](); qi += 1
        while pend_fin:
            pend_fin.pop(0)()
        if n == 9:
            alloc_Wv()
        if p == 2:
            first, second = fin_tasks(hp, hook=(a0_setup if hp == 3 else None))
            for t in first:
                t()
            if hp == 3:
                for t in second:
                    t()
            else:
                pend_fin = second
        cur_va = nva
    pipe.flush()

    Wv, b_Wv = A["Wv"], A["b_Wv"]
    vg_pool, _ = mkpool("vg", 8, [128, 1024], F32, P_BASE)
    sT, b_sT, off = ar.alloc("sT", [128, 8, T], F32, P_BASE + 32768)
    b_sTg = [Buf(f"sTg{g}") for g in range(4)]
    ar.live.extend((P_BASE + 32768, P_BASE + 32768 + 65536, b_) for b_ in b_sTg)
    A_TMP = off
    vn_pool, off = mkpool("vn", 6, [128, 1024], BF16, ACC0_BASE + 16384)
    sa_pool, off = mkpool("sta", 3, [128, 64], F32, off)
    assert off <= E_BASE and A_TMP <= ACC0_BASE
    WT, b_WT, cst, b_cst = A["WT"], A["b_WT"], A["cst"], A["b_cst"]

    A1S = {}

    def a1_s1(tc):
        G, k = tc // 4, tc % 4
        if k == 0:
            A1S[G] = {"st": sa_pool.get(), "vg": {}}
        st, b_st = A1S[G]["st"]
        vg, b_vg = A1S[G]["vg"][k] = vg_pool.get()
        for n in range(2):
            ps, b_ps = banks.get()
            mm_group(ps[:], b_ps, lambda kc: hTo[:, kc, tc * 128:(tc + 1) * 128],
                     lambda kc: Wv[:, kc, n * 512:(n + 1) * 512], 8, [b_hTo[tc // 4], b_Wv])
            c.op(ACT, lambda: sc.activation(vg[:, n * 512:(n + 1) * 512], ps[:], AF.Gelu_apprx_tanh),
                 reads=[b_ps], writes=[b_vg])
            c.op(DVE, lambda: ve.bn_stats(st[:, 12 * k + 6 * n:12 * k + 6 * n + 6], vg[:, n * 512:(n + 1) * 512]),
                 reads=[b_vg], writes=[b_st])
            banks.put((ps, b_ps))
        c.op(DVE, lambda: ve.bn_aggr(st[:, 56 + 2 * k:58 + 2 * k], st[:, 12 * k:12 * k + 12]), reads=[b_st], writes=[b_st])

    def a1_ln(G):
        st, b_st = A1S[G]["st"]
        c.op(ACT, lambda: sc.activation(st[:, 48:52], st[:, ssl(57, 4, 2)], AF.Ln, bias=eps_c, scale=1.0),
             reads=[b_st, b_cf], writes=[b_st])
        c.op(ACT, lambda: sc.activation(st[:, 52:56], st[:, 48:52], AF.Exp, scale=-0.5), reads=[b_st], writes=[b_st])

    A1V = {}

    def a1_norm(tc):
        G, k = tc // 4, tc % 4
        st, b_st = A1S[G]["st"]
        vg, b_vg = A1S[G]["vg"][k]
        vn, b_vn = A1V[tc] = vn_pool.get()
        c.op(DVE, lambda: ve.tensor_scalar(vn[:], vg[:], st[:, 56 + 2 * k:57 + 2 * k], st[:, 52 + k:53 + k],
                                           ALU.subtract, ALU.mult),
             reads=[b_vg, b_st], writes=[b_vn])
        vg_pool.put((vg, b_vg))
        if k == 3:
            sa_pool.put(A1S[G]["st"])

    def a1_spatial(tc):
        vn, b_vn = A1V[tc]
        pss = []
        for jb in range(2):
            ps, b_ps = banks.get()
            for jj in range(4):
                j = jb * 4 + jj
                c.op(PE, lambda: tn.matmul(ps[:, jj * 128:(jj + 1) * 128], vn[:, j * 128:(j + 1) * 128], WT[:, j // 2, :],
                                           start=True, stop=True),
                     reads=[b_vn, b_WT], writes=[b_ps], inc=(jj == 3))
            pss.append((ps, b_ps))
        vn_pool.put(A1V[tc])
        return pss

    def a1_evac(tc, pss):
        for jb in range(2):
            ps, b_ps = pss[jb]
            if jb == 0:
                c.op(ACT, lambda: sc.activation(sT[:, jb * 4:(jb + 1) * 4, tc * 128:(tc + 1) * 128],
                                                ps[:].rearrange("p (j i) -> p j i", j=4), AF.Copy),
                     reads=[b_ps], writes=[b_sTg[tc // 4]])
            else:
                c.op(DVE, lambda: ve.tensor_copy(sT[:, jb * 4:(jb + 1) * 4, tc * 128:(tc + 1) * 128],
                                                 ps[:].rearrange("p (j i) -> p j i", j=4)),
                     reads=[b_ps], writes=[b_sTg[tc // 4]])
            banks.put((ps, b_ps))

    A2 = {}

    def a2_alloc():
        A2["yaT"], A2["b_yaT"], _ = ar.alloc("yaT", [128, 8, T], BF16, P_BASE)
        A2["gu"], off = mkpool("gu", 2, [128, 512], F32, A_TMP)
        A2["t2"], off = mkpool("tg2", 2, [128, 512], F32, off)
        A2["m2"], off = mkpool("m2", 2, [128, 512], F32, off)
        A2["sa2"], off = mkpool("sa2", 2, [128, 512], F32, off)

    pcst = list(cst[:].ap[0])
    a2w = {}

    def a2_load(j):
        if j < 8 and j not in a2w:
            a2w[j] = (load_w(COL_U + j * 128), load_w(COL_G + j * 128))

    def a2_task(j, g):
        S = {}
        tsl = slice(g * 512, (g + 1) * 512)
        (wu, b_wu), (wg, b_wg) = a2w[j]

        def s1():
            pu, b_pu = S["pu"] = banks.get()
            mm_group(pu[:], b_pu, lambda kc: wu[:, kc, :], lambda kc: hTo[:, kc, tsl], 8, [b_wu, b_hTo[g]])
            pg, b_pg = S["pg"] = banks.get()
            mm_group(pg[:], b_pg, lambda kc: wg[:, kc, :], lambda kc: hTo[:, kc, tsl], 8, [b_wg, b_hTo[g]])
            gu, b_gu = S["gu"] = A2["gu"].get(); tg, b_tg = S["tg"] = A2["t2"].get()
            c.op(ACT, lambda: sc.activation(gu[:], pu[:], AF.Gelu_apprx_tanh), reads=[b_pu], writes=[b_gu])
            c.op(ACT, lambda: sc.activation(tg[:], pg[:], AF.Tanh, scale=0.5), reads=[b_pg], writes=[b_tg])
            banks.put(S["pu"])

        def s2():
            pg, b_pg = S["pg"]; gu, b_gu = S["gu"]; tg, b_tg = S["tg"]
            m2, b_m2 = A2["m2"].get()
            sa, b_sa = A2["sa2"].get()
            cstb = bass.AP(cst, j * 128, [pcst, [0, 4], [1, 128]])
            c.op(DVE, lambda: ve.scalar_tensor_tensor(sa[:].rearrange("p (a i) -> p a i", a=4),
                                                      sT[:, j, tsl].rearrange("p (a i) -> p a i", a=4),
                                                      cf_t[:, F_LNG + j:F_LNG + j + 1], cstb, ALU.mult, ALU.add),
                 reads=[b_sTg[g], b_cf, b_cst], writes=[b_sa])
            c.op(POOL, lambda: gp.tensor_tensor(gu[:], gu[:], sa[:], ALU.mult), reads=[b_gu, b_sa], writes=[b_gu])
            A2["sa2"].put((sa, b_sa))
            c.op(DVE, lambda: ve.scalar_tensor_tensor(m2[:], tg[:], 1.0, pg[:], ALU.add, ALU.mult),
                 reads=[b_tg, b_pg], writes=[b_m2])
            c.op(DVE, lambda: ve.scalar_tensor_tensor(A2["yaT"][:, j, tsl], gu[:], 0.5, m2[:], ALU.mult, ALU.mult),
                 reads=[b_gu, b_m2], writes=[A2["b_yaT"]])
            banks.put(S["pg"]); A2["gu"].put(S["gu"]); A2["t2"].put(S["tg"]); A2["m2"].put((m2, b_m2))

        pipe.submit(s1, s2, lag=1)

    c1w = {}

    def c1_load(j):
        if j < 8 and j not in c1w:
            c1w[j] = (load_w(COL_GA + j * 128), load_w(COL_GB + j * 128),
                      load_w(j * 128, src=w_oa), load_w(j * 128, src=w_ob, rows=4))

    a2_early = [(0, 0), (0, 1), (0, 2), (1, 0), (1, 1), (1, 2), (2, 0), (2, 1)]
    a2_done = set()
    a1_s1(0)
    a1_s1(1)
    a0_compute(0)
    a1_s1(2)
    a1_s1(3)
    a1_ln(0)
    a1_s1(4)
    a0_compute(1)
    a1_s1(5)
    for tc in range(4):
        a1_norm(tc)
    a1_s1(6)
    a0_compute(2)
    a1_s1(7)
    a0_compute(3)
    for G in range(4):
        for k in range(4):
            tc = 4 * G + k
            if k == 2 and G + 1 < 4:
                for t2 in range(4 * (G + 1), 4 * (G + 2)):
                    a1_norm(t2)
            if tc == 4:
                for j in range(3):
                    a2_load(j)
            if tc == 12:
                a2_alloc()
            pss = a1_spatial(tc)
            if tc + 8 < NCH:
                a1_s1(tc + 8)
            a1_evac(tc, pss)
            if k == 0 and G + 1 < 4:
                a1_ln(G + 1)
            if G == 3:
                for _ in range(2):
                    j, g = a2_early.pop(0)
                    a2_task(j, g)
                    a2_done.add((j, g))

    yaT, b_yaT = A2["yaT"], A2["b_yaT"]
    a2_load(0)
    for j in range(8):
        a2_load(j + 1)
        if j == 6:
            c1_load(0)
        for g in range(4):
            if (j, g) not in a2_done:
                a2_task(j, g)
    pipe.flush()

    off = P_BASE + 32768
    mT, b_mT, off = ar.alloc("mT", [128, 8, T], BF16, off)
    wout_t, b_wout, off = ar.alloc("wout", [128, 8, 1024], BF16, off)
    ta_pool, off = mkpool("ta", 2, [128, 512], F32, off)
    tb_pool, off = mkpool("tbb", 2, [128, 512], F32, off)
    m1_pool, off = mkpool("m1", 2, [128, 512], F32, off)
    m2c_pool, off = mkpool("m2c", 2, [128, 512], F32, off)
    def c1_task(j, g):
        S = {}
        tsl = slice(g * 512, (g + 1) * 512)
        (wga, b_wga), (wgb, b_wgb), (woa, b_woa), (wob, b_wob) = c1w[j]

        def s1():
            pga, b_pga = banks.get()
            mm_group(pga[:], b_pga, lambda kc: wga[:, kc, :], lambda kc: hTo[:, kc, tsl], 8, [b_wga, b_hTo[g]])
            pgb, b_pgb = banks.get()
            mm_group(pgb[:], b_pgb, lambda kc: wgb[:, kc, :], lambda kc: hTo[:, kc, tsl], 8, [b_wgb, b_hTo[g]])
            pya, b_pya = S["pya"] = banks.get()
            mm_group(pya[:], b_pya, lambda kc: woa[:, kc, :], lambda kc: yaT[:, kc, tsl], 8, [b_woa, b_yaT])
            pyb, b_pyb = S["pyb"] = banks.get()
            mm_group(pyb[:], b_pyb, lambda kc: wob[:, kc, :], lambda kc: ybT[:, kc, tsl], 4, [b_wob, b_ybT])
            ta, b_ta = S["ta"] = ta_pool.get(); tbb, b_tbb = S["tb"] = tb_pool.get()
            c.op(ACT, lambda: sc.activation(ta[:], pga[:], AF.Tanh, scale=0.5), reads=[b_pga], writes=[b_ta])
            c.op(ACT, lambda: sc.activation(tbb[:], pgb[:], AF.Tanh, scale=0.5), reads=[b_pgb], writes=[b_tbb])
            banks.put((pga, b_pga)); banks.put((pgb, b_pgb))

        def s2():
            pya, b_pya = S["pya"]; pyb, b_pyb = S["pyb"]; ta, b_ta = S["ta"]; tbb, b_tbb = S["tb"]
            m1, b_m1 = m1_pool.get(); m2, b_m2 = m2c_pool.get()
            c.op(DVE, lambda: ve.scalar_tensor_tensor(m1[:], ta[:], 1.0, pya[:], ALU.add, ALU.mult),
                 reads=[b_ta, b_pya], writes=[b_m1])
            c.op(DVE, lambda: ve.scalar_tensor_tensor(m2[:], tbb[:], 1.0, pyb[:], ALU.add, ALU.mult),
                 reads=[b_tbb, b_pyb], writes=[b_m2])
            c.op(POOL, lambda: gp.tensor_tensor(mT[:, j, tsl], m1[:], m2[:], ALU.add), reads=[b_m1, b_m2], writes=[b_mT])
            banks.put(S["pya"]); banks.put(S["pyb"]); ta_pool.put(S["ta"]); tb_pool.put(S["tb"])
            m1_pool.put((m1, b_m1)); m2c_pool.put((m2, b_m2))

        pipe.submit(s1, s2, lag=1)

    c1_load(0)
    for n in range(2):
        c.dma(POOL, wout_t[:, :, n * 512:(n + 1) * 512], w3(w_out, n * 512, 512), writes=[b_wout])
    for j in range(8):
        c1_load(j + 1)
        for g in range(4):
            c1_task(j, g)
    pipe.flush()

    xf_pool, off2 = mkpool("xf", 3, [128, D], F32, P_BASE)
    ot_pool, off2 = mkpool("ot", 3, [128, D], F32, off2)
    stores = []

    def fin_task(tc):
        S = {}
        rsl = slice(tc * 128, (tc + 1) * 128)

        def s1():
            xf, b_xf = S["xf"] = xf_pool.get()
            c.dma(ACT, xf[:], xo[rsl, :], writes=[b_xf])
            S["ps"] = []
            for n in range(2):
                ps, b_ps = banks.get()
                mm_group(ps[:], b_ps, lambda kc: mT[:, kc, rsl], lambda kc: wout_t[:, kc, n * 512:(n + 1) * 512], 8,
                         [b_mT, b_wout])
                S["ps"].append((ps, b_ps))

        def s2():
            xf, b_xf = S["xf"]
            ot, b_ot = ot_pool.get()
            for n in range(2):
                ps, b_ps = S["ps"][n]
                c.op(DVE, lambda: ve.scalar_tensor_tensor(ot[:, n * 512:(n + 1) * 512], ps[:], 0.5,
                                                          xf[:, n * 512:(n + 1) * 512], ALU.mult, ALU.add),
                     reads=[b_ps, b_xf], writes=[b_ot])
                banks.put((ps, b_ps))
            c.dma(SP, out[rsl, :], ot[:], reads=[b_ot])
            stores.append(b_ot)
            xf_pool.put(S["xf"]); ot_pool.put((ot, b_ot))

        pipe.submit(s1, s2, lag=1)

    for tc in range(NCH):
        fin_task(tc)
    pipe.flush()
    c.drain(SP, stores)
    c.close()
    if needed is None:
        return c.waited
    return nc


_CACHE = {}


def _host_consts(half):
    j = np.arange(128)[:, None]
    i = np.arange(128)[None, :]
    P = (j >= i).astype(np.float32)
    Cm = (j <= i).astype(np.float32)
    cb = np.zeros((128, C_W), np.float32)
    cb[:, C_MN:C_MN + 512] = np.concatenate([P, Cm, P, Cm], 1)
    flag = 1.0 if half == 1 else 0.0
    cb[:, C_MH:C_MH + 512] = np.concatenate([flag * P, Cm, flag * P, Cm], 1)
    cb[:, C_TRIL:C_TRIL + 128] = P
    cb[:, C_ID:C_ID + 128] = np.eye(128, dtype=np.float32)
    bd = np.zeros((128, 128), np.float32)
    bd[:64, :64] = 1.0
    bd[64:, 64:] = 1.0
    cb[:, C_BD:C_BD + 128] = bd
    cb[:, C_ONE:C_ONE + 128] = 1.0
    return cb


def kernel(x, norm_g, w_in, a_ws, a_bs, a_ln_g, a_ln_b, b_qn_g, b_kn_g, w_oa, w_ob, w_out):
    x = np.asarray(x, np.float32)
    f = lambda a: np.ascontiguousarray(np.asarray(a, np.float32))
    if "nc" not in _CACHE:
        _CACHE["nc"] = build_program(build_program())
    nc = _CACHE["nc"]
    cfa = np.zeros((128, F_W), np.float32)
    cfa[:, F_LNG:F_LNG + 8] = f(a_ln_g)[0].reshape(8, 128).T
    cfa[:, F_LNB:F_LNB + 8] = f(a_ln_b)[0].reshape(8, 128).T
    cfa[:, F_GQ:F_GQ + 3] = np.tile(f(b_qn_g)[0].T, (2, 1))
    cfa[:, F_GK:F_GK + 3] = np.tile(f(b_kn_g)[0].T, (2, 1))
    cfa[:, F_EPS] = EPS
    cfa[:, F_EPS64] = 64.0 * EPS
    shared = {
        "w_in": f(w_in)[0], "w_oa": f(w_oa)[0], "w_ob": f(w_ob)[0], "w_out": f(w_out)[0],
        "g_bc": np.ascontiguousarray(np.broadcast_to(f(norm_g)[0][None, :], (128, D))),
        "aws": np.ascontiguousarray(f(a_ws)[0].transpose(1, 0, 2)),
        "bsb": np.ascontiguousarray(np.broadcast_to(f(a_bs)[0][None], (128, 4, 128))),
        "cf": cfa,
    }
    cbs = [_host_consts(0), _host_consts(1)]
    zeros = np.zeros((T, D), np.float32)
    in_maps = []
    for core in range(8):
        b, half = core // 2, core % 2
        m = dict(shared)
        m["xo"] = np.ascontiguousarray(x[b, half * T:(half + 1) * T])
        m["xh"] = zeros if half == 0 else np.ascontiguousarray(x[b, 0:T])
        m["cbf"] = cbs[half]
        in_maps.append(m)
    res = run_bass_kernel_spmd(nc, in_maps, core_ids=list(range(8)))
    outp = np.empty((4, 2 * T, D), np.float32)
    for core in range(8):
        b, half = core // 2, core % 2
        outp[b, half * T:(half + 1) * T] = res.results[core]["out"]
    return outp
```

```python
import numpy as np
import ml_dtypes
import concourse.bass as bass
import concourse.mybir as mybir
from concourse.bass_utils import run_bass_kernel_spmd

F32 = mybir.dt.float32
BF16 = mybir.dt.bfloat16
ALU = mybir.AluOpType
AF = mybir.ActivationFunctionType


class Buf:
    __slots__ = ("name", "w", "r", "dsem", "dcnt")

    def __init__(self, name):
        self.name = name
        self.w = None
        self.r = []
        self.dsem = None
        self.dcnt = 0


class Eng:
    def __init__(self, name, h, sem):
        self.name = name
        self.h = h
        self.sem = sem
        self.cnt = 0
        self.seen = {}


class Ctx:
    def __init__(self, nc, needed=None):
        self.nc = nc
        self.needed = needed
        self.rank = None if needed is None else {k: {v: i + 1 for i, v in enumerate(sorted(vs))} for k, vs in needed.items()}
        self.waited = {}
        self.stack = []
        self.nsem = 0
        self.pe = self._eng("pe", nc.tensor)
        self.act = self._eng("act", nc.scalar)
        self.dve = self._eng("dve", nc.vector)
        self.pool = self._eng("pool", nc.gpsimd)
        self.sp = self._eng("sp", nc.sync)
        self.engs = [self.pe, self.act, self.dve, self.pool, self.sp]
        self.dbufs = []

    def new_sem(self, name):
        cm = self.nc.semaphore(name)
        s = cm.__enter__()
        self.stack.append(cm)
        self.nsem += 1
        return s

    def _eng(self, name, h):
        return Eng(name, h, self.new_sem("s_" + name))

    def close(self):
        for cm in reversed(self.stack):
            cm.__exit__(None, None, None)
        self.stack = []

    def _wait(self, eng, toks):
        best = {}
        for t in toks:
            if t is None:
                continue
            sem, val, src = t
            if src is self.pe and eng is self.pe:
                continue
            k = id(sem)
            if eng.seen.get(k, 0) >= val:
                continue
            if k not in best or best[k][1] < val:
                best[k] = (sem, val, src)
        for k, (sem, val, src) in best.items():
            if src is None:
                actual = val
            else:
                self.waited.setdefault(src.name, set()).add(val)
                actual = val if self.rank is None else self.rank[src.name][val]
            eng.h.wait_ge(sem, actual)
            eng.seen[k] = val

    def _deps(self, reads, writes):
        toks = []
        for b in reads:
            toks.append(b.w)
        for b in writes:
            toks.append(b.w)
            toks.extend(b.r)
        return toks

    def op(self, eng, fn, reads=(), writes=(), inc=True):
        self._wait(eng, self._deps(reads, writes))
        ins = fn()
        if inc:
            eng.cnt += 1
            if self.needed is None or eng.cnt in self.needed.get(eng.name, ()):
                ins.then_inc(eng.sem, 1)
            tok = (eng.sem, eng.cnt, eng)
        else:
            tok = (eng.sem, eng.cnt + 1, eng)
        for b in reads:
            b.r.append(tok)
        for b in writes:
            b.w = tok
            b.r = []
        return tok

    def dma(self, eng, out, in_, reads=(), writes=(), **kw):
        self._wait(eng, self._deps(reads, writes))
        owner = writes[0] if writes else reads[0]
        if owner.dsem is None:
            owner.dsem = self.new_sem("d_" + owner.name)
            self.dbufs.append(owner)
        owner.dcnt += 1
        eng.h.dma_start(out=out, in_=in_, **kw).then_inc(owner.dsem, 16)
        tok = (owner.dsem, 16 * owner.dcnt, None)
        for b in reads:
            b.r.append(tok)
        for b in writes:
            b.w = tok
            b.r = []
        return tok

    def barrier(self):
        toks = [(e.sem, e.cnt, e) for e in self.engs if e.cnt > 0]
        toks += [(b.dsem, 16 * b.dcnt, None) for b in self.dbufs]
        for e in self.engs:
            self._wait(e, [t for t in toks if t[2] is not e])

    def drain(self, eng, bufs):
        toks = []
        for b in bufs:
            toks.append(b.w)
            toks.extend(b.r)
        self._wait(eng, toks)


class Arena:
    BASE = 16512
    TOP = 229344

    def __init__(self, nc):
        self.nc = nc
        self.limit = self.TOP - self.BASE
        self.n = 0
        self.live = []

    def alloc(self, name, shape, dtype, off):
        esz = 2 if dtype == BF16 else 4
        nbytes = int(np.prod(shape[1:])) * esz
        assert off % 32 == 0, (name, off)
        end = off + nbytes
        assert end <= self.limit, (name, off, nbytes, self.limit)
        self.n += 1
        t = self.nc.alloc_sbuf_tensor_at(f"{name}_{self.n}", list(shape), dtype, offset=self.BASE + off)
        b = Buf(name)
        keep = []
        for (s0, e0, ob) in self.live:
            if s0 < end and off < e0:
                if ob.w is not None:
                    b.r.append(ob.w)
                b.r.extend(ob.r)
            else:
                keep.append((s0, e0, ob))
        keep.append((off, end, b))
        self.live = keep
        return t, b, off + ((nbytes + 31) // 32) * 32


class FreeList:
    def __init__(self, items):
        self.free = list(items)

    def get(self):
        assert self.free, "pool exhausted (pipeline too deep for the buffer count)"
        return self.free.pop(0)

    def put(self, it):
        self.free.append(it)


class Pipe:
    def __init__(self):
        self.i = 0
        self.pend = []

    def submit(self, s1, s2=None, lag=1, tag=None):
        s1()
        due = [e for e in self.pend if e[0] <= self.i]
        self.pend = [e for e in self.pend if e[0] > self.i]
        for e in due:
            e[1]()
        if s2 is not None:
            self.pend.append((self.i + lag, s2, tag))
        self.i += 1

    def flush(self, tag=None):
        keep = []
        for e in self.pend:
            if tag is None or e[2] == tag:
                e[1]()
            else:
                keep.append(e)
        self.pend = keep


D = 1024
T = 2048
NCH = T // 128
N_IN = 10240
EPS = 1e-6
PATTERNS = ((128, 1), (512, 4), (2048, 16))
COL_U, COL_V, COL_G = 0, 1024, 2048
COL_QKV = 3072
COL_BG = 3072 + 4608
COL_GA = COL_BG + 512
COL_GB = COL_GA + 1024
C_MN, C_MH, C_TRIL, C_ID, C_BD, C_ONE = 0, 512, 1024, 1152, 1280, 1408
C_W = 1536
F_LNG, F_LNB, F_GQ, F_GK, F_EPS, F_EPS64, F_W = 0, 8, 16, 19, 22, 23, 24


def ssl(start, n, step):
    return slice(start, start + (n - 1) * step + 1, step)


def build_program(needed=None):
    nc = bass.Bass("TRN2", target_bir_lowering=False)
    dt_in = lambda name, shape: nc.dram_tensor(name, list(shape), F32, kind="ExternalInput").ap()
    xo = dt_in("xo", (T, D))
    xh = dt_in("xh", (T, D))
    w_in = dt_in("w_in", (D, N_IN))
    w_oa = dt_in("w_oa", (D, D))
    w_ob = dt_in("w_ob", (512, D))
    w_out = dt_in("w_out", (D, D))
    g_bc = dt_in("g_bc", (128, D))
    aws = dt_in("aws", (128, 4, 128))
    bsb = dt_in("bsb", (128, 4, 128))
    cbf = nc.dram_tensor("cbf", [128, C_W], BF16, kind="ExternalInput").ap()
    cf = dt_in("cf", (128, F_W))
    out = nc.dram_tensor("out", [T, D], F32, kind="ExternalOutput").ap()

    c = Ctx(nc, needed)
    ar = Arena(nc)
    pipe = Pipe()
    PE, ACT, DVE, POOL, SP = c.pe, c.act, c.dve, c.pool, c.sp
    tn, sc, ve, gp = nc.tensor, nc.scalar, nc.vector, nc.gpsimd

    def w3(ap2d, c0, ncol):
        return ap2d.rearrange("(kc p) c -> p kc c", p=128)[:, :, c0:c0 + ncol]

    def mkpool(prefix, n, shape, dtype, off):
        items = []
        for i in range(n):
            t_, b_, off = ar.alloc(f"{prefix}{i}", shape, dtype, off)
            items.append((t_, b_))
        return FreeList(items), off

    banks = FreeList([(nc.alloc_psum_tensor(f"pb{i}", [128, 512], F32), Buf(f"pb{i}")) for i in range(8)])

    off = 0
    cb_t, b_cb, off = ar.alloc("cbf", [128, C_W], BF16, off)
    cf_t, b_cf, off = ar.alloc("cf", [128, F_W], F32, off)
    ybT, b_ybT, off = ar.alloc("ybT", [128, 4, T], BF16, off)
    hTo, _b_unused, off = ar.alloc("hTo", [128, 8, T], BF16, off)
    b_hTo = [Buf(f"hTo{g}") for g in range(4)]
    NW = 8
    wring = []
    for i in range(NW):
        t_, b_, off = ar.alloc(f"wr{i}", [128, 8, 128], BF16, off)
        wring.append((t_, b_))
    P_BASE = off
    wstate = {"i": 0}

    c.dma(SP, cf_t[:], cf, writes=[b_cf])
    maskN = cb_t[:, C_MN:C_MN + 512]
    maskH = cb_t[:, C_MH:C_MH + 512]
    trilW = cb_t[:, C_TRIL:C_TRIL + 128]
    ident = cb_t[:, C_ID:C_ID + 128]
    bdones = cb_t[:, C_BD:C_BD + 128]
    ones_m = cb_t[:, C_ONE:C_ONE + 128]
    eps_c = cf_t[:, F_EPS:F_EPS + 1]
    eps64_c = cf_t[:, F_EPS64:F_EPS64 + 1]

    def load_w(c0, src=None, rows=8):
        t_, b_ = wring[wstate["i"] % NW]
        wstate["i"] += 1
        src = w_in if src is None else src
        c.dma(POOL, t_[:, 0:rows, :], w3(src, c0, 128), writes=[b_])
        return t_, b_

    def mm_group(ps, pbuf, lhs_fn, rhs_fn, nk, reads, inc_last=True):
        for kc in range(nk):
            c.op(PE, lambda kc=kc: tn.matmul(ps, lhs_fn(kc), rhs_fn(kc), start=(kc == 0), stop=(kc == nk - 1)),
                 reads=reads, writes=[pbuf], inc=(kc == nk - 1 and inc_last))

    hTh, _b_unused2, off = ar.alloc("hTh", [128, 8, T], BF16, P_BASE)
    b_hTh = [Buf(f"hTh{i}") for i in range(NCH)]
    ar.live.extend((P_BASE, P_BASE + 32768, b_) for b_ in b_hTh)
    PB_BASE = off
    qT, b_qT, off = ar.alloc("qT", [128, T], BF16, off)
    kT, b_kT, off = ar.alloc("kT", [128, 2 * T], BF16, off)
    sq_pool, off = mkpool("sq", 4, [128, 512], BF16, off)
    lr_pool, off = mkpool("lr", 3, [128, 512], F32, off)
    Vaug, b_V, off = ar.alloc("Vaug", [128, 32, 2, 128], BF16, off)
    b_Vg = [Buf(f"Vg{g}") for g in range(8)]
    _vs = [e for e in ar.live if e[2] is b_V][0]
    ar.live.extend((_vs[0], _vs[1], b_) for b_ in b_Vg)
    c.op(DVE, lambda: ve.memset(Vaug[:], 1.0), writes=b_Vg)
    P0S_BASE = off
    gbc_t, b_gbc, off = ar.alloc("gbc", [128, D], F32, off)
    c.dma(ACT, gbc_t[:], g_bc, writes=[b_gbc])
    xt_pool, off = mkpool("xt", 4, [128, D], F32, off)
    junk, b_junk, off = ar.alloc("junk", [128, D], BF16, off)
    c.op(ACT, lambda: sc.activation(junk[:, 0:8], junk[:, 8:16], AF.Square), writes=[b_junk])
    hb_pool, off = mkpool("hb", 4, [128, D], BF16, off)
    st_pool, off = mkpool("st", 3, [128, 4], F32, off)
    ACC0_BASE = off
    acc, b_acc, off = ar.alloc("acc", [128, 2, T], F32, off)
    tg_pool, off = mkpool("tg", 2, [128, 512], F32, off)
    SG_BASE = off
    xtx_pool, off = mkpool("xtx", 2, [128, D], F32, off)
    P0 = {"sg": None}
    own_pool = FreeList(xt_pool.free + xtx_pool.free)
    ty_pool, off = mkpool("ty", 2, [128, 512], F32, off)
    E_BASE = off
    E_pool, off = mkpool("E", 4, [128, 2, 2, 2, 128], BF16, off)
    E_extra = {}
    for (_t, _b) in E_pool.free:
        _e = [e for e in ar.live if e[2] is _b][0]
        E_extra[id(_b)] = Buf(_b.name + "b")
        ar.live.append((_e[0], _e[1], E_extra[id(_b)]))
    pa = list(Vaug[:].ap[0])
    pacc = list(acc[:].ap[0])
    ACC = {0: (acc, b_acc), 1: (acc, b_acc)}
    vstate = {"i": 0}

    def p0_task(ci, halo):
        src = xh if halo else xo
        dstT = hTh if halo else hTo
        dstB = b_hTh[ci] if halo else b_hTo[ci // 4]
        S = {}

        def s1():
            xt, b_xt = S["xt"] = (xt_pool if halo else own_pool).get()
            hb, b_hb = S["hb"] = hb_pool.get()
            st, b_st = st_pool.get()
            c.dma(SP, xt[:], src[ci * 128:(ci + 1) * 128, :], writes=[b_xt])
            c.op(ACT, lambda: sc.activation(junk[:], xt[:], AF.Square, accum_out=st[:, 0:1]),
                 reads=[b_xt], writes=[b_junk, b_st])
            c.op(ACT, lambda: sc.activation(st[:, 1:2], st[:, 0:1], AF.Ln, bias=eps_c, scale=1.0 / D),
                 reads=[b_st, b_cf], writes=[b_st])
            c.op(ACT, lambda: sc.activation(st[:, 2:3], st[:, 1:2], AF.Exp, scale=-0.5),
                 reads=[b_st], writes=[b_st])
            c.op(DVE, lambda: ve.scalar_tensor_tensor(hb[:], xt[:], st[:, 2:3], gbc_t[:], ALU.mult, ALU.mult),
                 reads=[b_xt, b_st, b_gbc], writes=[b_hb])
            (xt_pool if halo else own_pool).put(S["xt"])
            st_pool.put((st, b_st))

        def s2():
            hb, b_hb = S["hb"]
            bk, b_bk = banks.get()
            tv = bk[:].bitcast(BF16).rearrange("p (k t) -> p k t", k=8)
            for kc in range(8):
                c.op(PE, lambda kc=kc: tn.transpose(tv[:, kc, :], hb[:, kc * 128:(kc + 1) * 128], ident),
                     reads=[b_hb, b_cb], writes=[b_bk], inc=(kc == 7))
            c.op(DVE, lambda: ve.tensor_copy(dstT[:, :, ci * 128:(ci + 1) * 128], tv), reads=[b_bk], writes=[dstB])
            hb_pool.put(S["hb"])
            banks.put((bk, b_bk))

        return lambda: pipe.submit(s1, s2, lag=(4 if (halo and ci < 15) else 2))

    def hTh_bufs(t0, n):
        return [b_hTh[i] for i in range(t0 // 128, (t0 + n - 1) // 128 + 1)]

    def hTo_bufs(t0, n):
        return [b_hTo[i] for i in range(t0 // 512, (t0 + n - 1) // 512 + 1)]

    QKLAG = [2]

    def qk_task(src_fn, w_t, b_w, srcBs, n, d, gcol, dst, dstB):
        S = {}

        def s1():
            ps, b_ps = S["ps"] = banks.get()
            mm_group(ps[:, 0:n], b_ps, lambda kc: w_t[:, kc, :], src_fn, 8, [b_w] + srcBs)
            sq, b_sq = S["sq"] = sq_pool.get()
            c.op(ACT, lambda: sc.activation(sq[:, 0:n], ps[:, 0:n], AF.Square), reads=[b_ps], writes=[b_sq])

        def s2():
            ps, b_ps = S["ps"]
            sq, b_sq = S["sq"]
            ps2, b_ps2 = banks.get()
            lr, b_lr = lr_pool.get()
            c.op(PE, lambda: tn.matmul(ps2[:, 0:n], bdones, sq[:, 0:n], start=True, stop=True),
                 reads=[b_sq, b_cb], writes=[b_ps2])
            c.op(ACT, lambda: sc.activation(lr[:, 0:n], ps2[:, 0:n], AF.Ln, bias=eps64_c, scale=1.0),
                 reads=[b_ps2, b_cf], writes=[b_lr])
            c.op(ACT, lambda: sc.activation(lr[:, 0:n], lr[:, 0:n], AF.Exp, scale=-0.5), reads=[b_lr], writes=[b_lr])
            pv_ = ps[:, 0:n].rearrange("p (j r) -> p r j", r=d)
            rv_ = lr[:, 0:n].rearrange("p (j r) -> p r j", r=d)
            c.op(DVE, lambda: ve.scalar_tensor_tensor(dst, pv_, gcol, rv_, ALU.mult, ALU.mult),
                 reads=[b_ps, b_lr, b_cf], writes=[dstB])
            banks.put(S["ps"]); banks.put((ps2, b_ps2)); sq_pool.put(S["sq"]); lr_pool.put((lr, b_lr))

        return lambda: pipe.submit(s1, s2, lag=QKLAG[0], tag="qk")

    stepw = {}

    def step_w(hp, p):
        if (hp, p) not in stepw:
            cq = COL_QKV + p * 1536 + hp * 128
            stepw[(hp, p)] = (load_w(cq), load_w(cq + 512), load_w(cq + 1024))
        return stepw[(hp, p)]

    finw = {}

    def fin_w(hp):
        if hp not in finw:
            finw[hp] = load_w(COL_BG + hp * 128)
        return finw[hp]

    vkinds = set()

    QK = {0: (qT, b_qT, kT, b_kT)}

    def step_tasks(hp, p, prefetch=True, buf=0, att_lag=3):
        qT, b_qT, kT, b_kT = QK[buf]
        acc, b_acc = ACC[hp % 2]
        tasks = []
        va = []
        if p == 2:
            fin_w(hp)
        if True:
            win, d = PATTERNS[p]
            nbo = NCH // d
            LK = 128 + T // d
            (wq, b_wq), (wk, b_wk), (wv, b_wv) = step_w(hp, p)
            nxt = hp * 3 + p + 1
            if nxt < 12 and prefetch:
                step_w(nxt // 3, nxt % 3)
            qv = qT[:].rearrange("p (r j) -> p r j", r=d)
            kv = kT[:, 0:d * LK].rearrange("p (r j) -> p r j", r=d)
            gq = cf_t[:, F_GQ + p:F_GQ + p + 1]
            gk = cf_t[:, F_GK + p:F_GK + p + 1]
            for g in range(4):
                j0 = g * 512 // d
                tasks.append(qk_task(lambda kc, g=g: hTo[:, kc, g * 512:(g + 1) * 512], wq, b_wq, [b_hTo[g]], 512, d, gq,
                                     qv[:, :, j0:j0 + 512 // d], b_qT))
            nh = 128 * d
            for s0 in range(0, nh, 512):
                n = min(512, nh - s0)
                t0 = T - nh + s0
                j0 = s0 // d
                tasks.append(qk_task(lambda kc, t0=t0, n=n: hTh[:, kc, t0:t0 + n], wk, b_wk, hTh_bufs(t0, n), n, d, gk,
                                     kv[:, :, j0:j0 + n // d], b_kT))
            for g in range(4):
                j0 = 128 + g * 512 // d
                tasks.append(qk_task(lambda kc, g=g: hTo[:, kc, g * 512:(g + 1) * 512], wk, b_wk, [b_hTo[g]], 512, d, gk,
                                     kv[:, :, j0:j0 + 512 // d], b_kT))
            blocks = [(r, cb) for r in range(d) for cb in range(nbo + 1)]
            vtasks, atasks = [], []
            for b0 in range(0, len(blocks), 4):
                def v_s1(b0=b0):
                    grpb = blocks[b0:b0 + 4]
                    ps, b_ps = banks.get()
                    for bi, (r, cb) in enumerate(grpb):
                        if cb == 0:
                            srcT, t0 = hTh, T - 128 * d + r
                            srcBs = hTh_bufs(T - 128 * d, 128 * d)
                        else:
                            srcT, t0 = hTo, r + 128 * d * (cb - 1)
                            srcBs = hTo_bufs(128 * d * (cb - 1), 128 * d)
                        mm_group(ps[:, bi * 128:(bi + 1) * 128], b_ps,
                                 lambda kc: srcT[:, kc, ssl(t0, 128, d)], lambda kc: wv[:, kc, :], 8, [b_wv] + srcBs,
                                 inc_last=(bi == len(grpb) - 1))
                    nb = len(grpb)
                    dst = bass.AP(Vaug, b0 * 256, [pa, [256, nb], [192, 2], [1, 64]])
                    srcv = ps[:, 0:nb * 128].rearrange("p (b h e) -> p b h e", b=nb, h=2)
                    c.op(DVE, lambda: ve.tensor_copy(dst, srcv), reads=[b_ps], writes=[b_Vg[b0 // 4]])
                    banks.put((ps, b_ps))
                _f = (lambda v_s1=v_s1: pipe.submit(v_s1))
                vkinds.add(id(_f))
                vtasks.append(_f)
            qblocks = [(r, cc) for r in range(d) for cc in range(nbo)]
            for q0 in range(0, len(qblocks), 2):
                def att(q0=q0):
                    pair = qblocks[q0:q0 + 2]
                    S = {}

                    def s1():
                        E, b_E0 = S["E"] = E_pool.get()
                        b_Eb = [b_E0, E_extra[id(b_E0)]]
                        sb = [banks.get(), banks.get()]
                        for bi, (r, cc) in enumerate(pair):
                            for pc in range(2):
                                for h in range(2):
                                    rows = slice(64 * h, 64 * h + 64)
                                    k0 = 128 * (cc + pc)
                                    o_ = (bi * 2 + pc) * 128
                                    c.op(PE, lambda: tn.matmul(sb[h][0][:, o_:o_ + 128], kv[rows, r, k0:k0 + 128],
                                                               qv[rows, r, 128 * cc:128 * cc + 128], start=True, stop=True),
                                         reads=[b_kT, b_qT], writes=[sb[h][1]], inc=(bi == len(pair) - 1 and pc == 1))
                        for h in range(2):
                            c.op(ACT, lambda: sc.activation(E[:, h].rearrange("p b c q -> p (b c q)"), sb[h][0][:],
                                                            AF.Exp, scale=8.0),
                                 reads=[sb[h][1]], writes=b_Eb)
                        for bi, (r, cc) in enumerate(pair):
                            mk = maskH if cc == 0 else maskN
                            ev = E[:, :, bi].rearrange("p h c q -> p h (c q)")
                            if False:
                                c.op(POOL, lambda: gp.tensor_tensor(ev, ev, mk.rearrange("p (h x) -> p h x", h=2), ALU.mult),
                                     reads=[b_Eb[bi], b_cb], writes=[b_Eb[bi]])
                            else:
                                c.op(DVE, lambda: ve.tensor_tensor(ev, ev, mk.rearrange("p (h x) -> p h x", h=2), ALU.mult),
                                     reads=[b_Eb[bi], b_cb], writes=[b_Eb[bi]])
                        banks.put(sb[0]); banks.put(sb[1])

                    def s2():
                        E, b_E0 = S["E"]
                        b_Eb = [b_E0, E_extra[id(b_E0)]]
                        po, b_po = banks.get()
                        for bi, (r, cc) in enumerate(pair):
                            for h in range(2):
                                for pc in range(2):
                                    blk = r * (nbo + 1) + cc + pc
                                    o_ = (bi * 2 + h) * 128
                                    c.op(PE, lambda: tn.matmul(po[:, o_:o_ + 128], Vaug[:, blk, h, :], E[:, h, bi, pc, :],
                                                               start=(pc == 0), stop=(pc == 1)),
                                         reads=[b_Vg[blk // 4], b_Eb[bi]], writes=[b_po],
                                         inc=(pc == 1 and h == 1 and bi == len(pair) - 1))
                        (r0, c0), (r1, c1) = pair
                        t00 = r0 + 128 * d * c0
                        t01 = r1 + 128 * d * c1
                        av = bass.AP(acc, t00, [pacc, [t01 - t00, 2], [T, 2], [d, 128]])
                        ov = po[:].rearrange("p (b h q) -> p b h q", b=2, h=2)
                        if p == 0:
                            c.op(DVE, lambda: ve.tensor_copy(av, ov), reads=[b_po], writes=[b_acc])
                        else:
                            c.op(DVE, lambda: ve.tensor_tensor(av, ov, av, ALU.add), reads=[b_po, b_acc], writes=[b_acc])
                        E_pool.put(S["E"]); banks.put((po, b_po))

                    need = max(r * (nbo + 1) + cc + 1 for (r, cc) in pair) // 4
                    return need, (lambda: pipe.submit(s1, s2, lag=att_lag, tag="att"))
                atasks.append(att())
            nv = 0
            for need, a in atasks:
                while nv < len(vtasks) and nv <= need:
                    va.append(vtasks[nv]); nv += 1
                va.append(a)
            while nv < len(vtasks):
                va.append(vtasks[nv]); nv += 1
        return tasks, va

    def fin_tasks(hp, hook=None):
        wb, b_wb = fin_w(hp)
        acc, b_acc = ACC[hp % 2]
        SS = [dict() for _ in range(4)]
        first, second = [], []
        for g in range(4):
            def s1(g=g):
                S = SS[g]
                tsl = slice(g * 512, (g + 1) * 512)
                ps, b_ps = banks.get()
                mm_group(ps[:], b_ps, lambda kc: wb[:, kc, :], lambda kc: hTo[:, kc, tsl], 8, [b_wb, b_hTo[g]])
                tg, b_tg = tg_pool.get()
                sg, b_sg = S["sg"] = P0["sg"].get()
                c.op(ACT, lambda: sc.activation(tg[:], ps[:], AF.Tanh, scale=0.5), reads=[b_ps], writes=[b_tg])
                c.op(DVE, lambda: ve.scalar_tensor_tensor(sg[:], tg[:], 1.0, ps[:], ALU.add, ALU.mult),
                     reads=[b_tg, b_ps], writes=[b_sg])
                banks.put((ps, b_ps)); tg_pool.put((tg, b_tg))
            first.append(lambda s1=s1: pipe.submit(s1))
        first.append(lambda: pipe.flush(tag="att"))
        if hook is not None:
            first.append(hook)
        for g in range(4):
            def s2(g=g):
                S = SS[g]
                tsl = slice(g * 512, (g + 1) * 512)
                sg, b_sg = S["sg"]
                ty, b_ty = ty_pool.get()
                c.op(ACT, lambda: sc.activation(ty[0:64, :], acc[64:128, 0, tsl], AF.Ln), reads=[b_acc], writes=[b_ty])
                c.op(ACT, lambda: sc.activation(ty[64:128, :], acc[0:64, 1, tsl], AF.Ln), reads=[b_acc], writes=[b_ty])
                c.op(ACT, lambda: sc.activation(ty[:], ty[:], AF.Exp, scale=-1.0), reads=[b_ty], writes=[b_ty])
                c.op(DVE, lambda: ve.tensor_tensor(ty[0:64, :], ty[0:64, :], acc[0:64, 0, tsl], ALU.mult),
                     reads=[b_acc, b_ty], writes=[b_ty])
                c.op(DVE, lambda: ve.tensor_tensor(ty[64:128, :], ty[64:128, :], acc[64:128, 1, tsl], ALU.mult),
                     reads=[b_acc, b_ty], writes=[b_ty])
                c.op(DVE, lambda: ve.scalar_tensor_tensor(ybT[:, hp, tsl], ty[:], 0.5, sg[:], ALU.mult, ALU.mult),
                     reads=[b_ty, b_sg], writes=[b_ybT])
                P0["sg"].put(S["sg"]); ty_pool.put((ty, b_ty))
            second.append(lambda s2=s2: pipe.submit(s2))
        return first, second

    c.dma(ACT, cb_t[:], cbf, writes=[b_cb])
    own = [p0_task(ci, False) for ci in range(NCH)]
    halo = {hc: p0_task(hc, True) for hc in range(NCH)}
    q00, va00 = step_tasks(0, 0, prefetch=False)
    seq = [own[0], own[1], own[2], own[3], own[4], own[5], own[6], own[7], q00[0],
           own[8], own[9], own[10], own[11], q00[1],
           own[12], own[13], own[14], own[15], q00[2],
           halo[15], halo[14], q00[3]]
    for i, t in enumerate(seq):
        t()
    step_w(0, 1)
    P0["sg"], _ = mkpool("sg", 4, [128, 512], F32, SG_BASE)
    pend_halo = [13, 12]
    for i, t in enumerate(q00[4:] + va00):
        t()
        if i % 4 == 3 and pend_halo:
            halo[pend_halo.pop(0)]()
    while pend_halo:
        halo[pend_halo.pop(0)]()
    pend_halo = list(range(11, -1, -1))
    q01, va01 = step_tasks(0, 1)
    for i, t in enumerate(q01 + va01):
        t()
        if i % 2 == 1 and pend_halo:
            halo[pend_halo.pop(0)]()
    while pend_halo:
        halo[pend_halo.pop(0)]()
    pipe.flush()
    qT1, b_qT1, off2 = ar.alloc("qT1", [128, T], BF16, P0S_BASE)
    kT1, b_kT1, off2 = ar.alloc("kT1", [128, 2 * T], BF16, off2)
    QK[1] = (qT1, b_qT1, kT1, b_kT1)
    acc1, b_acc1, off2 = ar.alloc("acc1", [128, 2, T], F32, off2)
    ACC[1] = (acc1, b_acc1)

    A = {}

    def alloc_Wv():
        A["Wv"], A["b_Wv"], _ = ar.alloc("Wv", [128, 8, 1024], BF16, ACC0_BASE)
        for n in range(2):
            c.dma(POOL, A["Wv"][:, :, n * 512:(n + 1) * 512], w3(w_in, COL_V + n * 512, 512), writes=[A["b_Wv"]])

    def a0_setup():
        pipe.flush()
        WT, b_WT, off = ar.alloc("WT", [128, 4, 128], BF16, E_BASE)
        cst, b_cst, off = ar.alloc("cst", [128, 8, 128], F32, off)
        aws_t, b_aws, off = ar.alloc("aws", [128, 4, 128], F32, PB_BASE)
        bsb_t, b_bsb, off = ar.alloc("bsb", [128, 4, 128], F32, off)
        wm_t, b_wm, off = ar.alloc("wm", [128, 4, 128], BF16, off)
        A.update(WT=WT, b_WT=b_WT, cst=cst, b_cst=b_cst)
        c.dma(SP, aws_t[:], aws, writes=[b_aws])
        c.dma(SP, bsb_t[:], bsb, writes=[b_bsb])
        A["a0"] = (WT, b_WT, cst, b_cst, aws_t, b_aws, bsb_t, b_bsb, wm_t, b_wm)

    def a0_compute(part):
        WT, b_WT, cst, b_cst, aws_t, b_aws, bsb_t, b_bsb, wm_t, b_wm = A["a0"]
        if part == 0:
            for g in range(4):
                c.op(DVE, lambda: ve.tensor_tensor(wm_t[:, g, :], aws_t[:, g, :], trilW, ALU.mult),
                     reads=[b_aws, b_cb], writes=[b_wm])
        elif part == 1:
            tbk, b_tbk = banks.get()
            tbv = tbk[:].bitcast(BF16).rearrange("p (k t) -> p k t", k=8)
            for g in range(4):
                c.op(PE, lambda: tn.transpose(tbv[:, g, :], wm_t[:, g, :], ident), reads=[b_wm, b_cb], writes=[b_tbk],
                     inc=(g == 3))
            c.op(DVE, lambda: ve.tensor_copy(WT[:], tbv[:, 0:4, :]), reads=[b_tbk], writes=[b_WT])
            banks.put((tbk, b_tbk))
        elif part == 2:
            ps, b_ps = A["a0ps"] = banks.get()
            for g in range(4):
                c.op(PE, lambda: tn.matmul(ps[:, g * 128:(g + 1) * 128], ones_m, WT[:, g, :], start=True, stop=True),
                     reads=[b_WT, b_cb], writes=[b_ps], inc=(g == 3))
        else:
            ps, b_ps = A["a0ps"]
            for j in range(8):
                g = j // 2
                c.op(DVE, lambda: ve.scalar_tensor_tensor(cst[:, j, :], ps[:, g * 128:(g + 1) * 128],
                                                          cf_t[:, F_LNB + j:F_LNB + j + 1], bsb_t[:, g, :], ALU.mult, ALU.add),
                     reads=[b_ps, b_cf, b_bsb], writes=[b_cst])
            banks.put((ps, b_ps))

    steps = [(n // 3, n % 3) for n in range(12)]
    QKLAG[0] = 2
    cur_q, cur_va = step_tasks(0, 2, buf=0, att_lag=5)
    for t in cur_q:
        t()
    pend_fin = []
    for n in range(2, 12):
        hp, p = steps[n]
        if n + 1 < 12:
            nq, nva = step_tasks(steps[n + 1][0], steps[n + 1][1], buf=(n + 1) % 2, att_lag=(5 if n + 2 < 12 else 3))
        else:
            nq, nva = [], []
        lastv = max(i for i, t in enumerate(cur_va) if id(t) in vkinds)
        pipe.flush(tag="qk")
        qi = 0
        for i, t in enumerate(cur_va):
            t()
            if qi < len(nq):
                nq[qi](); qi += 1
            if i % 2 == 1 and i >= 3 and pend_fin:
                pend_fin.pop(0)()
        while qi < len(nq):
            nq[qi](); qi += 1
        while pend_fin:
            pend_fin.pop(0)()
        if n == 9:
            alloc_Wv()
        if p == 2:
            first, second = fin_tasks(hp, hook=(a0_setup if hp == 3 else None))
            for t in first:
                t()
            if hp == 3:
                for t in second:
                    t()
            else:
                pend_fin = second
        cur_va = nva
    pipe.flush()

    Wv, b_Wv = A["Wv"], A["b_Wv"]
    vg_pool, _ = mkpool("vg", 8, [128, 1024], F32, P_BASE)
    sT, b_sT, off = ar.alloc("sT", [128, 8, T], F32, P_BASE + 32768)
    b_sTg = [Buf(f"sTg{g}") for g in range(4)]
    ar.live.extend((P_BASE + 32768, P_BASE + 32768 + 65536, b_) for b_ in b_sTg)
    A_TMP = off
    vn_pool, off = mkpool("vn", 6, [128, 1024], BF16, ACC0_BASE + 16384)
    sa_pool, off = mkpool("sta", 3, [128, 64], F32, off)
    assert off <= E_BASE and A_TMP <= ACC0_BASE
    WT, b_WT, cst, b_cst = A["WT"], A["b_WT"], A["cst"], A["b_cst"]

    A1S = {}

    def a1_s1(tc):
        G, k = tc // 4, tc % 4
        if k == 0:
            A1S[G] = {"st": sa_pool.get(), "vg": {}}
        st, b_st = A1S[G]["st"]
        vg, b_vg = A1S[G]["vg"][k] = vg_pool.get()
        for n in range(2):
            ps, b_ps = banks.get()
            mm_group(ps[:], b_ps, lambda kc: hTo[:, kc, tc * 128:(tc + 1) * 128],
                     lambda kc: Wv[:, kc, n * 512:(n + 1) * 512], 8, [b_hTo[tc // 4], b_Wv])
            c.op(ACT, lambda: sc.activation(vg[:, n * 512:(n + 1) * 512], ps[:], AF.Gelu_apprx_tanh),
                 reads=[b_ps], writes=[b_vg])
            c.op(DVE, lambda: ve.bn_stats(st[:, 12 * k + 6 * n:12 * k + 6 * n + 6], vg[:, n * 512:(n + 1) * 512]),
                 reads=[b_vg], writes=[b_st])
            banks.put((ps, b_ps))
        c.op(DVE, lambda: ve.bn_aggr(st[:, 56 + 2 * k:58 + 2 * k], st[:, 12 * k:12 * k + 12]), reads=[b_st], writes=[b_st])

    def a1_ln(G):
        st, b_st = A1S[G]["st"]
        c.op(ACT, lambda: sc.activation(st[:, 48:52], st[:, ssl(57, 4, 2)], AF.Ln, bias=eps_c, scale=1.0),
             reads=[b_st, b_cf], writes=[b_st])
        c.op(ACT, lambda: sc.activation(st[:, 52:56], st[:, 48:52], AF.Exp, scale=-0.5), reads=[b_st], writes=[b_st])

    A1V = {}

    def a1_norm(tc):
        G, k = tc // 4, tc % 4
        st, b_st = A1S[G]["st"]
        vg, b_vg = A1S[G]["vg"][k]
        vn, b_vn = A1V[tc] = vn_pool.get()
        c.op(DVE, lambda: ve.tensor_scalar(vn[:], vg[:], st[:, 56 + 2 * k:57 + 2 * k], st[:, 52 + k:53 + k],
                                           ALU.subtract, ALU.mult),
             reads=[b_vg, b_st], writes=[b_vn])
        vg_pool.put((vg, b_vg))
        if k == 3:
            sa_pool.put(A1S[G]["st"])

    def a1_spatial(tc):
        vn, b_vn = A1V[tc]
        pss = []
        for jb in range(2):
            ps, b_ps = banks.get()
            for jj in range(4):
                j = jb * 4 + jj
                c.op(PE, lambda: tn.matmul(ps[:, jj * 128:(jj + 1) * 128], vn[:, j * 128:(j + 1) * 128], WT[:, j // 2, :],
                                           start=True, stop=True),
                     reads=[b_vn, b_WT], writes=[b_ps], inc=(jj == 3))
            pss.append((ps, b_ps))
        vn_pool.put(A1V[tc])
        return pss

    def a1_evac(tc, pss):
        for jb in range(2):
            ps, b_ps = pss[jb]
            if jb == 0:
                c.op(ACT, lambda: sc.activation(sT[:, jb * 4:(jb + 1) * 4, tc * 128:(tc + 1) * 128],
                                                ps[:].rearrange("p (j i) -> p j i", j=4), AF.Copy),
                     reads=[b_ps], writes=[b_sTg[tc // 4]])
            else:
                c.op(DVE, lambda: ve.tensor_copy(sT[:, jb * 4:(jb + 1) * 4, tc * 128:(tc + 1) * 128],
                                                 ps[:].rearrange("p (j i) -> p j i", j=4)),
                     reads=[b_ps], writes=[b_sTg[tc // 4]])
            banks.put((ps, b_ps))

    A2 = {}

    def a2_alloc():
        A2["yaT"], A2["b_yaT"], _ = ar.alloc("yaT", [128, 8, T], BF16, P_BASE)
        A2["gu"], off = mkpool("gu", 2, [128, 512], F32, A_TMP)
        A2["t2"], off = mkpool("tg2", 2, [128, 512], F32, off)
        A2["m2"], off = mkpool("m2", 2, [128, 512], F32, off)
        A2["sa2"], off = mkpool("sa2", 2, [128, 512], F32, off)

    pcst = list(cst[:].ap[0])
    a2w = {}

    def a2_load(j):
        if j < 8 and j not in a2w:
            a2w[j] = (load_w(COL_U + j * 128), load_w(COL_G + j * 128))

    def a2_task(j, g):
        S = {}
        tsl = slice(g * 512, (g + 1) * 512)
        (wu, b_wu), (wg, b_wg) = a2w[j]

        def s1():
            pu, b_pu = S["pu"] = banks.get()
            mm_group(pu[:], b_pu, lambda kc: wu[:, kc, :], lambda kc: hTo[:, kc, tsl], 8, [b_wu, b_hTo[g]])
            pg, b_pg = S["pg"] = banks.get()
            mm_group(pg[:], b_pg, lambda kc: wg[:, kc, :], lambda kc: hTo[:, kc, tsl], 8, [b_wg, b_hTo[g]])
            gu, b_gu = S["gu"] = A2["gu"].get(); tg, b_tg = S["tg"] = A2["t2"].get()
            c.op(ACT, lambda: sc.activation(gu[:], pu[:], AF.Gelu_apprx_tanh), reads=[b_pu], writes=[b_gu])
            c.op(ACT, lambda: sc.activation(tg[:], pg[:], AF.Tanh, scale=0.5), reads=[b_pg], writes=[b_tg])
            banks.put(S["pu"])

        def s2():
            pg, b_pg = S["pg"]; gu, b_gu = S["gu"]; tg, b_tg = S["tg"]
            m2, b_m2 = A2["m2"].get()
            sa, b_sa = A2["sa2"].get()
            cstb = bass.AP(cst, j * 128, [pcst, [0, 4], [1, 128]])
            c.op(DVE, lambda: ve.scalar_tensor_tensor(sa[:].rearrange("p (a i) -> p a i", a=4),
                                                      sT[:, j, tsl].rearrange("p (a i) -> p a i", a=4),
                                                      cf_t[:, F_LNG + j:F_LNG + j + 1], cstb, ALU.mult, ALU.add),
                 reads=[b_sTg[g], b_cf, b_cst], writes=[b_sa])
            c.op(POOL, lambda: gp.tensor_tensor(gu[:], gu[:], sa[:], ALU.mult), reads=[b_gu, b_sa], writes=[b_gu])
            A2["sa2"].put((sa, b_sa))
            c.op(DVE, lambda: ve.scalar_tensor_tensor(m2[:], tg[:], 1.0, pg[:], ALU.add, ALU.mult),
                 reads=[b_tg, b_pg], writes=[b_m2])
            c.op(DVE, lambda: ve.scalar_tensor_tensor(A2["yaT"][:, j, tsl], gu[:], 0.5, m2[:], ALU.mult, ALU.mult),
                 reads=[b_gu, b_m2], writes=[A2["b_yaT"]])
            banks.put(S["pg"]); A2["gu"].put(S["gu"]); A2["t2"].put(S["tg"]); A2["m2"].put((m2, b_m2))

        pipe.submit(s1, s2, lag=1)

    c1w = {}

    def c1_load(j):
        if j < 8 and j not in c1w:
            c1w[j] = (load_w(COL_GA + j * 128), load_w(COL_GB + j * 128),
                      load_w(j * 128, src=w_oa), load_w(j * 128, src=w_ob, rows=4))

    a2_early = [(0, 0), (0, 1), (0, 2), (1, 0), (1, 1), (1, 2), (2, 0), (2, 1)]
    a2_done = set()
    a1_s1(0)
    a1_s1(1)
    a0_compute(0)
    a1_s1(2)
    a1_s1(3)
    a1_ln(0)
    a1_s1(4)
    a0_compute(1)
    a1_s1(5)
    for tc in range(4):
        a1_norm(tc)
    a1_s1(6)
    a0_compute(2)
    a1_s1(7)
    a0_compute(3)
    for G in range(4):
        for k in range(4):
            tc = 4 * G + k
            if k == 2 and G + 1 < 4:
                for t2 in range(4 * (G + 1), 4 * (G + 2)):
                    a1_norm(t2)
            if tc == 4:
                for j in range(3):
                    a2_load(j)
            if tc == 12:
                a2_alloc()
            pss = a1_spatial(tc)
            if tc + 8 < NCH:
                a1_s1(tc + 8)
            a1_evac(tc, pss)
            if k == 0 and G + 1 < 4:
                a1_ln(G + 1)
            if G == 3:
                for _ in range(2):
                    j, g = a2_early.pop(0)
                    a2_task(j, g)
                    a2_done.add((j, g))

    yaT, b_yaT = A2["yaT"], A2["b_yaT"]
    a2_load(0)
    for j in range(8):
        a2_load(j + 1)
        if j == 6:
            c1_load(0)
        for g in range(4):
            if (j, g) not in a2_done:
                a2_task(j, g)
    pipe.flush()

    off = P_BASE + 32768
    mT, b_mT, off = ar.alloc("mT", [128, 8, T], BF16, off)
    wout_t, b_wout, off = ar.alloc("wout", [128, 8, 1024], BF16, off)
    ta_pool, off = mkpool("ta", 2, [128, 512], F32, off)
    tb_pool, off = mkpool("tbb", 2, [128, 512], F32, off)
    m1_pool, off = mkpool("m1", 2, [128, 512], F32, off)
    m2c_pool, off = mkpool("m2c", 2, [128, 512], F32, off)
    def c1_task(j, g):
        S = {}
        tsl = slice(g * 512, (g + 1) * 512)
        (wga, b_wga), (wgb, b_wgb), (woa, b_woa), (wob, b_wob) = c1w[j]

        def s1():
            pga, b_pga = banks.get()
            mm_group(pga[:], b_pga, lambda kc: wga[:, kc, :], lambda kc: hTo[:, kc, tsl], 8, [b_wga, b_hTo[g]])
            pgb, b_pgb = banks.get()
            mm_group(pgb[:], b_pgb, lambda kc: wgb[:, kc, :], lambda kc: hTo[:, kc, tsl], 8, [b_wgb, b_hTo[g]])
            pya, b_pya = S["pya"] = banks.get()
            mm_group(pya[:], b_pya, lambda kc: woa[:, kc, :], lambda kc: yaT[:, kc, tsl], 8, [b_woa, b_yaT])
            pyb, b_pyb = S["pyb"] = banks.get()
            mm_group(pyb[:], b_pyb, lambda kc: wob[:, kc, :], lambda kc: ybT[:, kc, tsl], 4, [b_wob, b_ybT])
            ta, b_ta = S["ta"] = ta_pool.get(); tbb, b_tbb = S["tb"] = tb_pool.get()
            c.op(ACT, lambda: sc.activation(ta[:], pga[:], AF.Tanh, scale=0.5), reads=[b_pga], writes=[b_ta])
            c.op(ACT, lambda: sc.activation(tbb[:], pgb[:], AF.Tanh, scale=0.5), reads=[b_pgb], writes=[b_tbb])
            banks.put((pga, b_pga)); banks.put((pgb, b_pgb))

        def s2():
            pya, b_pya = S["pya"]; pyb, b_pyb = S["pyb"]; ta, b_ta = S["ta"]; tbb, b_tbb = S["tb"]
            m1, b_m1 = m1_pool.get(); m2, b_m2 = m2c_pool.get()
            c.op(DVE, lambda: ve.scalar_tensor_tensor(m1[:], ta[:], 1.0, pya[:], ALU.add, ALU.mult),
                 reads=[b_ta, b_pya], writes=[b_m1])
            c.op(DVE, lambda: ve.scalar_tensor_tensor(m2[:], tbb[:], 1.0, pyb[:], ALU.add, ALU.mult),
                 reads=[b_tbb, b_pyb], writes=[b_m2])
            c.op(POOL, lambda: gp.tensor_tensor(mT[:, j, tsl], m1[:], m2[:], ALU.add), reads=[b_m1, b_m2], writes=[b_mT])
            banks.put(S["pya"]); banks.put(S["pyb"]); ta_pool.put(S["ta"]); tb_pool.put(S["tb"])
            m1_pool.put((m1, b_m1)); m2c_pool.put((m2, b_m2))

        pipe.submit(s1, s2, lag=1)

    c1_load(0)
    for n in range(2):
        c.dma(POOL, wout_t[:, :, n * 512:(n + 1) * 512], w3(w_out, n * 512, 512), writes=[b_wout])
    for j in range(8):
        c1_load(j + 1)
        for g in range(4):
            c1_task(j, g)
    pipe.flush()

    xf_pool, off2 = mkpool("xf", 3, [128, D], F32, P_BASE)
    ot_pool, off2 = mkpool("ot", 3, [128, D], F32, off2)
    stores = []

    def fin_task(tc):
        S = {}
        rsl = slice(tc * 128, (tc + 1) * 128)

        def s1():
            xf, b_xf = S["xf"] = xf_pool.get()
            c.dma(ACT, xf[:], xo[rsl, :], writes=[b_xf])
            S["ps"] = []
            for n in range(2):
                ps, b_ps = banks.get()
                mm_group(ps[:], b_ps, lambda kc: mT[:, kc, rsl], lambda kc: wout_t[:, kc, n * 512:(n + 1) * 512], 8,
                         [b_mT, b_wout])
                S["ps"].append((ps, b_ps))

        def s2():
            xf, b_xf = S["xf"]
            ot, b_ot = ot_pool.get()
            for n in range(2):
                ps, b_ps = S["ps"][n]
                c.op(DVE, lambda: ve.scalar_tensor_tensor(ot[:, n * 512:(n + 1) * 512], ps[:], 0.5,
                                                          xf[:, n * 512:(n + 1) * 512], ALU.mult, ALU.add),
                     reads=[b_ps, b_xf], writes=[b_ot])
                banks.put((ps, b_ps))
            c.dma(SP, out[rsl, :], ot[:], reads=[b_ot])
            stores.append(b_ot)
            xf_pool.put(S["xf"]); ot_pool.put((ot, b_ot))

        pipe.submit(s1, s2, lag=1)

    for tc in range(NCH):
        fin_task(tc)
    pipe.flush()
    c.drain(SP, stores)
    c.close()
    if needed is None:
        return c.waited
    return nc


_CACHE = {}


def _host_consts(half):
    j = np.arange(128)[:, None]
    i = np.arange(128)[None, :]
    P = (j >= i).astype(np.float32)
    Cm = (j <= i).astype(np.float32)
    cb = np.zeros((128, C_W), np.float32)
    cb[:, C_MN:C_MN + 512] = np.concatenate([P, Cm, P, Cm], 1)
    flag = 1.0 if half == 1 else 0.0
    cb[:, C_MH:C_MH + 512] = np.concatenate([flag * P, Cm, flag * P, Cm], 1)
    cb[:, C_TRIL:C_TRIL + 128] = P
    cb[:, C_ID:C_ID + 128] = np.eye(128, dtype=np.float32)
    bd = np.zeros((128, 128), np.float32)
    bd[:64, :64] = 1.0
    bd[64:, 64:] = 1.0
    cb[:, C_BD:C_BD + 128] = bd
    cb[:, C_ONE:C_ONE + 128] = 1.0
    return cb


def kernel(x, norm_g, w_in, a_ws, a_bs, a_ln_g, a_ln_b, b_qn_g, b_kn_g, w_oa, w_ob, w_out):
    x = np.asarray(x, np.float32)
    f = lambda a: np.ascontiguousarray(np.asarray(a, np.float32))
    if "nc" not in _CACHE:
        _CACHE["nc"] = build_program(build_program())
    nc = _CACHE["nc"]
    cfa = np.zeros((128, F_W), np.float32)
    cfa[:, F_LNG:F_LNG + 8] = f(a_ln_g)[0].reshape(8, 128).T
    cfa[:, F_LNB:F_LNB + 8] = f(a_ln_b)[0].reshape(8, 128).T
    cfa[:, F_GQ:F_GQ + 3] = np.tile(f(b_qn_g)[0].T, (2, 1))
    cfa[:, F_GK:F_GK + 3] = np.tile(f(b_kn_g)[0].T, (2, 1))
    cfa[:, F_EPS] = EPS
    cfa[:, F_EPS64] = 64.0 * EPS
    shared = {
        "w_in": f(w_in)[0], "w_oa": f(w_oa)[0], "w_ob": f(w_ob)[0], "w_out": f(w_out)[0],
        "g_bc": np.ascontiguousarray(np.broadcast_to(f(norm_g)[0][None, :], (128, D))),
        "aws": np.ascontiguousarray(f(a_ws)[0].transpose(1, 0, 2)),
        "bsb": np.ascontiguousarray(np.broadcast_to(f(a_bs)[0][None], (128, 4, 128))),
        "cf": cfa,
    }
    cbs = [_host_consts(0).astype(ml_dtypes.bfloat16), _host_consts(1).astype(ml_dtypes.bfloat16)]
    zeros = np.zeros((T, D), np.float32)
    in_maps = []
    for core in range(8):
        b, half = core // 2, core % 2
        m = dict(shared)
        m["xo"] = np.ascontiguousarray(x[b, half * T:(half + 1) * T])
        m["xh"] = zeros if half == 0 else np.ascontiguousarray(x[b, 0:T])
        m["cbf"] = cbs[half]
        in_maps.append(m)
    res = run_bass_kernel_spmd(nc, in_maps, core_ids=list(range(8)))
    outp = np.empty((4, 2 * T, D), np.float32)
    for core in range(8):
        b, half = core // 2, core % 2
        outp[b, half * T:(half + 1) * T] = res.results[core]["out"]
    return outp
```
